# Optimizing a Trainium2 kernel written in Bass

```python
import math
import jax, jax.numpy as jnp
from jax import lax
import numpy as np

D_MODEL = 1024
BATCH = 32
SEQ = 2048
DEPTH = 1

D_MIX = D_MODEL
HEAD_DIM = 64
N_MLA_HEADS = D_MIX // 2 // HEAD_DIM
N_DIL_HEADS = D_MIX // 2 // HEAD_DIM
MLA_NOPE_DIM = HEAD_DIM
MLA_ROPE_DIM = HEAD_DIM // 2
MLA_V_DIM = HEAD_DIM
MLA_Q_LORA = 3 * D_MODEL // 8
MLA_KV_LORA = D_MODEL // 4
D_MLA_OUT = N_MLA_HEADS * MLA_V_DIM
D_DIL_OUT = N_DIL_HEADS * HEAD_DIM
D_IN_PROJ = MLA_Q_LORA + MLA_KV_LORA + MLA_ROPE_DIM + 3 * D_DIL_OUT
IN_SPLITS = (MLA_Q_LORA, MLA_Q_LORA + MLA_KV_LORA, MLA_Q_LORA + MLA_KV_LORA + MLA_ROPE_DIM)
DIL_PAIRS = ((128, 1), (512, 4), (2048, 16))
PARTIAL_ROT_DIM = HEAD_DIM // 4
ROPE_THETA = 500000.0
D_FF = ((8 * D_MODEL // 3 + 255) // 256) * 256
Q_BLOCK = 128
NORM_EPS = 1e-6
MASK_VALUE = -1e30

kernel_name = "hybrid_mla_dilated_macaron_encoder"


def rms_norm(x, g):
    xf = x.astype(jnp.float32)
    y = xf * lax.rsqrt(jnp.mean(xf * xf, axis=-1, keepdims=True) + NORM_EPS)
    return (y * g.astype(jnp.float32)).astype(x.dtype)


def swiglu(x, w_gate, w_up, w_down):
    return (jax.nn.silu(x @ w_gate) * (x @ w_up)) @ w_down


def rope_tables(seq, rot_dim, dtype):
    half = rot_dim // 2
    inv_freq = ROPE_THETA ** (-jnp.arange(half, dtype=jnp.float32) * (2.0 / rot_dim))
    ang = jnp.arange(seq, dtype=jnp.float32)[:, None] * inv_freq[None, :]
    return jnp.cos(ang).astype(dtype), jnp.sin(ang).astype(dtype)


def apply_rope(x, cos, sin):
    half = cos.shape[-1]
    c = cos[:, None, :]
    s = sin[:, None, :]
    x1 = x[..., :half]
    x2 = x[..., half:2 * half]
    return jnp.concatenate([x1 * c - x2 * s, x2 * c + x1 * s, x[..., 2 * half:]], axis=-1)


def dense_attention(q, k, v):
    b, s, h, dk = q.shape
    nb = s // Q_BLOCK
    scale = dk ** -0.5
    qb = q.reshape(b, nb, Q_BLOCK, h, dk).transpose(1, 0, 2, 3, 4)

    def one_block(qblk):
        sc = jnp.einsum('bqhd,bkhd->bhqk', qblk, k, preferred_element_type=jnp.float32) * scale
        p = jax.nn.softmax(sc, axis=-1)
        return jnp.einsum('bhqk,bkhd->bqhd', p.astype(v.dtype), v)

    out = lax.map(one_block, qb)
    return out.transpose(1, 0, 2, 3, 4).reshape(b, s, h, v.shape[-1])


def banded_attention(q, k, v, half):
    L, dh = q.shape[-2], q.shape[-1]
    lead = q.shape[:-2]
    blk = min(Q_BLOCK, L)
    nb = -(-L // blk)
    lp = nb * blk
    span = blk + 2 * half
    pad_lead = [(0, 0)] * len(lead)
    qp = jnp.pad(q, pad_lead + [(0, lp - L), (0, 0)])
    kp = jnp.pad(k, pad_lead + [(half, lp - L + half), (0, 0)])
    vp = jnp.pad(v, pad_lead + [(half, lp - L + half), (0, 0)])
    idx = np.arange(nb)[:, None] * blk + np.arange(span)[None, :]
    kb = jnp.take(kp, idx, axis=-2)
    vb = jnp.take(vp, idx, axis=-2)
    qb = qp.reshape(*lead, nb, blk, dh)
    sc = jnp.einsum('...nqd,...nkd->...nqk', qb, kb, preferred_element_type=jnp.float32) * (dh ** -0.5)
    key_pos = idx - half
    t = np.arange(span)[None, :]
    qi = np.arange(blk)[:, None]
    band = (t >= qi) & (t <= qi + 2 * half)
    valid = (key_pos >= 0) & (key_pos < L)
    mask = band[None, :, :] & valid[:, None, :]
    sc = jnp.where(mask, sc, MASK_VALUE)
    m = jnp.max(sc, axis=-1, keepdims=True)
    p = jnp.exp(sc - m)
    denom = jnp.sum(p, axis=-1, keepdims=True)
    o = jnp.einsum('...nqk,...nkd->...nqd', (p / denom).astype(v.dtype), vb)
    lse = (m + jnp.log(denom))[..., 0]
    o = o.reshape(*lead, lp, dh)[..., :L, :]
    lse = lse.reshape(*lead, lp)[..., :L]
    return o, lse


def dilated_attention(q, k, v):
    b, s, h, dh = q.shape
    outs, lses = [], []
    for window, dil in DIL_PAIRS:
        L = s // dil

        def strided(t):
            return t.reshape(b, L, dil, h, dh).transpose(0, 2, 3, 1, 4)

        o, lse = banded_attention(strided(q), strided(k), strided(v), window // (2 * dil))
        outs.append(o.transpose(0, 3, 1, 2, 4).reshape(b, s, h, dh))
        lses.append(lse.transpose(0, 3, 1, 2).reshape(b, s, h))
    w = jax.nn.softmax(jnp.stack(lses, axis=0), axis=0)
    return jnp.einsum('nbsh,nbshd->bshd', w.astype(q.dtype), jnp.stack(outs, axis=0))


def _dense(key, fan_in, fan_out):
    return jax.random.normal(key, (DEPTH, fan_in, fan_out), jnp.float32) * fan_in ** -0.5


def _gain(key, dim):
    return 1.0 + 0.02 * jax.random.normal(key, (DEPTH, dim), jnp.float32)


def setup_inputs(seed: int = 0) -> dict:
    key = jax.random.key(seed)
    ks = jax.random.split(key, 20)
    return {
        "x": jax.random.normal(ks[0], (BATCH, SEQ, D_MODEL), jnp.float32),
        "ffn1_norm": _gain(ks[1], D_MODEL),
        "ffn1_w_gate": _dense(ks[2], D_MODEL, D_FF),
        "ffn1_w_up": _dense(ks[3], D_MODEL, D_FF),
        "ffn1_w_down": _dense(ks[4], D_FF, D_MODEL),
        "mix_norm": _gain(ks[5], D_MODEL),
        "w_in": _dense(ks[6], D_MODEL, D_IN_PROJ),
        "mla_q_norm": _gain(ks[7], MLA_Q_LORA),
        "mla_w_uq": _dense(ks[8], MLA_Q_LORA, N_MLA_HEADS * (MLA_NOPE_DIM + MLA_ROPE_DIM)),
        "mla_kv_norm": _gain(ks[9], MLA_KV_LORA),
        "mla_w_ukv": _dense(ks[10], MLA_KV_LORA, N_MLA_HEADS * (MLA_NOPE_DIM + MLA_V_DIM)),
        "mla_out_norm": _gain(ks[11], D_MLA_OUT),
        "dil_out_norm": _gain(ks[12], D_DIL_OUT),
        "w_out": _dense(ks[13], D_MIX, D_MODEL),
        "ffn2_norm": _gain(ks[14], D_MODEL),
        "ffn2_w_gate": _dense(ks[15], D_MODEL, D_FF),
        "ffn2_w_up": _dense(ks[16], D_MODEL, D_FF),
        "ffn2_w_down": _dense(ks[17], D_FF, D_MODEL),
        "final_norm": 1.0 + 0.02 * jax.random.normal(ks[18], (D_MODEL,), jnp.float32),
    }


def reference(x, ffn1_norm, ffn1_w_gate, ffn1_w_up, ffn1_w_down, mix_norm, w_in,
              mla_q_norm, mla_w_uq, mla_kv_norm, mla_w_ukv, mla_out_norm, dil_out_norm,
              w_out, ffn2_norm, ffn2_w_gate, ffn2_w_up, ffn2_w_down, final_norm):
    b, s, _ = x.shape
    cos_p, sin_p = rope_tables(s, PARTIAL_ROT_DIM, x.dtype)
    cos_m, sin_m = rope_tables(s, MLA_ROPE_DIM, x.dtype)
    for l in range(DEPTH):
        x = x + 0.5 * swiglu(rms_norm(x, ffn1_norm[l]), ffn1_w_gate[l], ffn1_w_up[l], ffn1_w_down[l])

        h = rms_norm(x, mix_norm[l])
        proj = h @ w_in[l]
        c_q, c_kv, k_rope, qkv_dil = jnp.split(proj, IN_SPLITS, axis=-1)

        q = (rms_norm(c_q, mla_q_norm[l]) @ mla_w_uq[l]).reshape(b, s, N_MLA_HEADS, MLA_NOPE_DIM + MLA_ROPE_DIM)
        q_mla = jnp.concatenate([q[..., :MLA_NOPE_DIM], apply_rope(q[..., MLA_NOPE_DIM:], cos_m, sin_m)], axis=-1)
        kv = (rms_norm(c_kv, mla_kv_norm[l]) @ mla_w_ukv[l]).reshape(b, s, N_MLA_HEADS, MLA_NOPE_DIM + MLA_V_DIM)
        k_nope = kv[..., :MLA_NOPE_DIM]
        v_mla = kv[..., MLA_NOPE_DIM:]
        k_r = apply_rope(k_rope[:, :, None, :], cos_m, sin_m)
        k_mla = jnp.concatenate([k_nope, jnp.broadcast_to(k_r, (b, s, N_MLA_HEADS, MLA_ROPE_DIM))], axis=-1)
        o_mla = dense_attention(q_mla, k_mla, v_mla).reshape(b, s, D_MLA_OUT)

        qkv = qkv_dil.reshape(b, s, 3, N_DIL_HEADS, HEAD_DIM)
        q_d = apply_rope(qkv[:, :, 0], cos_p, sin_p)
        k_d = apply_rope(qkv[:, :, 1], cos_p, sin_p)
        v_d = qkv[:, :, 2]
        o_dil = dilated_attention(q_d, k_d, v_d).reshape(b, s, D_DIL_OUT)

        mixed = jnp.concatenate([rms_norm(o_mla, mla_out_norm[l]), rms_norm(o_dil, dil_out_norm[l])], axis=-1)
        x = x + mixed @ w_out[l]

        x = x + 0.5 * swiglu(rms_norm(x, ffn2_norm[l]), ffn2_w_gate[l], ffn2_w_up[l], ffn2_w_down[l])
    return rms_norm(x, final_norm)
```

```python
import math
from contextlib import ExitStack

import numpy as np
import ml_dtypes

import concourse.bass as bass
import concourse.mybir as mybir
from concourse.bass_utils import run_bass_kernel_spmd

F32 = mybir.dt.float32
BF16 = mybir.dt.bfloat16
AF = mybir.ActivationFunctionType
ALU = mybir.AluOpType

N_CORES = 8
SEQ = 2048
DM = 1024
DFF = 2816
NT = SEQ // 128
NB = SEQ // 512
KC = DM // 128
FC = DFF // 128
EPS = 1e-6
ROPE_THETA = 500000.0

G1, GMIX, GQ, GKV, GMO, GDO, G2 = 0, 8, 16, 19, 21, 25, 29
NG = 37


class Prog:
    ENGS = ("pe", "act", "dve", "pool", "sp")

    def __init__(self, nc):
        self.nc = nc
        self.E = {"pe": nc.tensor, "act": nc.scalar, "dve": nc.vector, "pool": nc.gpsimd, "sp": nc.sync}
        self.ops = []
        self.last_w = {}
        self.readers = {}
        self.bar = set()
        self.last_eng = {}
        self.last_chan = {}

    def _add(self, eng, fn, r, w, chan=None):
        i = len(self.ops)
        deps = set(self.bar)
        for k in r:
            if k in self.last_w:
                deps.add(self.last_w[k])
        for k in w:
            if k in self.last_w:
                deps.add(self.last_w[k])
            deps.update(self.readers.get(k, ()))
        for k in r:
            self.readers.setdefault(k, []).append(i)
        for k in w:
            self.last_w[k] = i
            self.readers[k] = []
        self.ops.append([eng, fn, deps, chan, None])
        if chan is None:
            self.last_eng[eng] = i
        else:
            self.last_chan[chan] = i
        return i

    def op(self, eng, fn, r=(), w=()):
        return self._add(eng, fn, r, w)

    def dma(self, fn, r=(), w=(), chan=None, q="sp"):
        assert chan is not None
        return self._add(q, fn, r, w, chan)

    def barrier(self):
        self.bar = set(self.last_eng.values()) | set(self.last_chan.values())

    def emit(self, stack):
        nc = self.nc
        ops = self.ops
        has_dep = [False] * len(ops)
        for o in ops:
            for d in o[2]:
                has_dep[d] = True
        cnt = {e: 0 for e in self.ENGS}
        ccnt = {}
        for i, o in enumerate(ops):
            if o[3] is not None:
                ccnt[o[3]] = ccnt.get(o[3], 0) + 16
                o[4] = (("c", o[3]), ccnt[o[3]])
            elif has_dep[i]:
                cnt[o[0]] += 1
                o[4] = (("e", o[0]), cnt[o[0]])
        sems = {}
        for e in self.ENGS:
            if cnt[e]:
                sems[("e", e)] = stack.enter_context(nc.semaphore("s_" + e))
        for c in ccnt:
            sems[("c", c)] = stack.enter_context(nc.semaphore("c_" + c))
        waited = {e: {} for e in self.ENGS}
        for i, o in enumerate(ops):
            eng, fn, deps, chan, sig = o
            need = {}
            for d in deps:
                od = ops[d]
                if od[3] is None and od[0] == eng and eng in ("pe", "sp"):
                    continue
                s, v = od[4]
                if need.get(s, 0) < v:
                    need[s] = v
            E = self.E[eng]
            for s, v in need.items():
                if waited[eng].get(s, 0) >= v:
                    continue
                waited[eng][s] = v
                E.wait_ge(sems[s], v)
            ins = fn()
            if sig is not None:
                ins.then_inc(sems[sig[0]], 16 if chan is not None else 1)
        self.stats = (len(ops), dict(cnt), dict(ccnt))


class Arena:
    def __init__(self, t, n):
        self.t, self.n, self.off = t, n, 0
        self.peak = 0

    def reset(self, off=0):
        self.off = off

    def _take(self, n):
        a = self.off
        self.off += n
        self.peak = max(self.peak, self.off)
        assert self.off <= self.n, ("arena overflow", self.off, self.n)
        return a

    def bf(self, *shape):
        n = int(np.prod(shape))
        a = self._take(n)
        ap = self.t[:, a:a + n]
        if len(shape) == 2:
            return ap.rearrange("p (a b) -> p a b", a=shape[0])
        if len(shape) == 3:
            return ap.rearrange("p (a b c) -> p a b c", a=shape[0], b=shape[1])
        return ap

    def f32(self, *shape):
        self.off = (self.off + 1) // 2 * 2
        n = int(np.prod(shape))
        a = self._take(2 * n)
        ap = self.t[:, a:a + 2 * n].bitcast(F32)
        if len(shape) == 2:
            return ap.rearrange("p (a b) -> p a b", a=shape[0])
        return ap


def build_program(nseq, dbg_stage=None):
    nc = bass.Bass("TRN2", target_bir_lowering=False)
    P = Prog(nc)

    def din(name, shape, dt=F32):
        return nc.dram_tensor(name, list(shape), dt, kind="ExternalInput").ap()

    def dscr(name, shape):
        return nc.dram_tensor(name, list(shape), BF16, kind="Internal").ap()

    x_d = din("x", [nseq, SEQ, DM])
    y_d = nc.dram_tensor("y", [nseq, SEQ, DM], F32, kind="ExternalOutput").ap()
    wsrc = {
        "wg1": din("wg1", [DM, DFF]), "wu1": din("wu1", [DM, DFF]), "wd1": din("wd1", [DFF, DM]),
        "wg2": din("wg2", [DM, DFF]), "wu2": din("wu2", [DM, DFF]), "wd2": din("wd2", [DFF, DM]),
        "wlat": din("wlat", [DM, 704]), "wuq": din("wuq", [384, 1024]),
        "wuk": din("wuk", [256, 512]), "wuv": din("wuv", [256, 512]),
        "wdqk": din("wdqk", [DM, 1280]), "wdv": din("wdv", [DM, 512]), "wout": din("wout", [DM, DM]),
    }
    wgain = {"wg1": G1, "wu1": G1, "wd1": None, "wg2": G2, "wu2": G2, "wd2": None, "wlat": GMIX, "wuq": GQ,
             "wuk": GKV, "wuv": GKV, "wdqk": GMIX, "wdv": GMIX, "wout": GMO}
    wb = {k: dscr(k + "_b", v.shape) for k, v in wsrc.items()}
    gains_d = din("gains", [128, NG])
    gfin_d = din("gfin", [1, DM])
    tabm_d = din("tabm", [128, SEQ])
    tabd_d = din("tabd", [128, SEQ])
    mask_d = din("mask", [128, 256], BF16)
    ident_d = din("ident", [128, 128], BF16)

    stack = ExitStack()
    with stack:
        sb = lambda name, shape, dt: stack.enter_context(nc.sbuf_tensor("sb_" + name, shape, dt))
        xs = sb("xs", [128, NT, DM], F32)
        hT = sb("hT", [128, KC, SEQ], BF16)
        ident = sb("ident", [128, 128], BF16)
        ones_b = sb("ones_b", [128, 8], BF16)
        ones_f = sb("ones_f", [128, 64], F32)
        nhalf = sb("nhalf", [128, 16], F32)
        gains = sb("gains", [128, NG], F32)
        mask = sb("mask", [128, 256], BF16)
        stat = sb("stat", [128, 3, 16], F32)
        rstd_g = sb("rstd_g", [128, 3, 16], F32)
        ARENA_N = 51800
        arena_t = sb("arena", [128, ARENA_N], BF16)
        ps = stack.enter_context(nc.psum_tensor("ps", [128, 8 * 512], F32))
        A = Arena(arena_t, ARENA_N)

        def bank(b, n=512, rows=128, r0=0):
            return ps[r0:r0 + rows, b * 512:b * 512 + n]

        def bank_bf(b, n=1024):
            return ps[:, b * 512:(b + 1) * 512].bitcast(BF16)[:, 0:n]

        PB = lambda b: ("ps", b)

        P.dma(lambda: nc.sync.dma_start(out=ident[:], in_=ident_d[:, :]), w=["ident"], chan="k0")
        P.dma(lambda: nc.sync.dma_start(out=gains[:], in_=gains_d[:, :]), w=["gains"], chan="k1")
        P.dma(lambda: nc.sync.dma_start(out=mask[:], in_=mask_d[:, :]), w=["mask"], chan="k2")
        P.op("pool", lambda: nc.gpsimd.memset(ones_b[:], 1.0), w=["ones_b"])
        P.op("pool", lambda: nc.gpsimd.memset(ones_f[:], 1.0), w=["ones_f"])
        P.op("pool", lambda: nc.gpsimd.memset(nhalf[:], -0.5), w=["nhalf"])

        A.reset()
        stg_f = A.f32(2, DFF)
        stg_b = A.bf(2, DFF)
        it = 0
        for name, src in wsrc.items():
            K, N = src.shape
            for j in range(K // 128):
                s = it % 2
                gcol = wgain[name]
                if name == "wout" and j >= 4:
                    gcol = GDO - 4
                P.dma(lambda s=s, src=src, j=j, N=N: nc.sync.dma_start(out=stg_f[:, s, 0:N], in_=src[j * 128:(j + 1) * 128, :]),
                      w=[("stf", s)], chan="il%d" % s)
                if gcol is None:
                    if it % 2 == 0:
                        P.op("act", lambda s=s, N=N: nc.scalar.copy(out=stg_b[:, s, 0:N], in_=stg_f[:, s, 0:N]),
                             r=[("stf", s)], w=[("stb", s)])
                    else:
                        P.op("dve", lambda s=s, N=N: nc.vector.tensor_copy(out=stg_b[:, s, 0:N], in_=stg_f[:, s, 0:N]),
                             r=[("stf", s)], w=[("stb", s)])
                else:
                    gc = gcol + j
                    if it % 2 == 0:
                        P.op("act", lambda s=s, N=N, gc=gc: nc.scalar.activation(out=stg_b[:, s, 0:N], in_=stg_f[:, s, 0:N],
                                                                                 func=AF.Copy, scale=gains[:, gc:gc + 1]),
                             r=[("stf", s), "gains"], w=[("stb", s)])
                    else:
                        P.op("dve", lambda s=s, N=N, gc=gc: nc.vector.tensor_scalar(out=stg_b[:, s, 0:N], in0=stg_f[:, s, 0:N],
                                                                                    scalar1=gains[:, gc:gc + 1], scalar2=None,
                                                                                    op0=ALU.mult),
                             r=[("stf", s), "gains"], w=[("stb", s)])
                P.dma(lambda s=s, name=name, j=j, N=N: nc.sync.dma_start(out=wb[name][j * 128:(j + 1) * 128, :], in_=stg_b[:, s, 0:N]),
                      r=[("stb", s)], w=[("W", name)], chan="is%d" % s)
                it += 1
        P.barrier()

        def MM(out, pairs, start=True, skip=False):
            def fn():
                ins = None
                n = len(pairs)
                for i, (l, r) in enumerate(pairs):
                    ins = nc.tensor.matmul(out, lhsT=l, rhs=r, start=(start and i == 0), stop=(i == n - 1),
                                           skip_group_check=skip)
                return ins
            return fn

        def MMS(items):
            def fn():
                ins = None
                for (o, l, r, st, sp) in items:
                    ins = nc.tensor.matmul(o, lhsT=l, rhs=r, start=st, stop=sp, skip_group_check=True)
                return ins
            return fn

        def ACT(out, in_, func, **kw):
            return lambda: nc.scalar.activation(out=out, in_=in_, func=func, **kw)

        def ACOPY(out, in_):
            return lambda: nc.scalar.copy(out=out, in_=in_)

        def TT(eng, out, in0, in1, op):
            e = nc.vector if eng == "dve" else nc.gpsimd
            return lambda: e.tensor_tensor(out=out, in0=in0, in1=in1, op=op)

        def TS(out, in0, s1, s2, op0, op1=None):
            if op1 is None:
                return lambda: nc.vector.tensor_scalar(out=out, in0=in0, scalar1=s1, scalar2=None, op0=op0)
            return lambda: nc.vector.tensor_scalar(out=out, in0=in0, scalar1=s1, scalar2=s2, op0=op0, op1=op1)

        def STT(out, in0, scalar, in1, op0, op1):
            return lambda: nc.vector.scalar_tensor_tensor(out=out, in0=in0, scalar=scalar, in1=in1, op0=op0, op1=op1)

        def VCOPY(out, in_):
            return lambda: nc.vector.tensor_copy(out=out, in_=in_)

        def RECIP(out, in_):
            return lambda: nc.vector.reciprocal(out=out, in_=in_)

        def DMA(out, in_):
            return lambda: nc.sync.dma_start(out=out, in_=in_)

        statc = [0]

        def rms_rstd(src_ap, n, junk_ap, src_res, junk_res):
            c = statc[0] % 16
            statc[0] += 1
            P.op("act", ACT(junk_ap, src_ap, AF.Square, accum_out=stat[:, 0, c:c + 1]), r=src_res, w=junk_res + [("st0", c)])
            P.op("dve", TS(stat[:, 1, c:c + 1], stat[:, 0, c:c + 1], 1.0 / n, EPS, ALU.mult, ALU.add), r=[("st0", c)], w=[("st1", c)])
            P.op("pool", TT("pool", stat[:, 2, c:c + 1], stat[:, 1, c:c + 1], nhalf[:, 0:1], ALU.pow),
                 r=[("st1", c), "nhalf"], w=[("st2", c)])
            return stat[:, 2, c:c + 1], ("st2", c)

        tb = [0]

        def norm_T(src_ap, n, src_res, xn_ap, xn_res, dstT, dst_res, tok0, evac_eng):
            nch = n // 128
            rstd, rres = rms_rstd(src_ap, n, xn_ap, src_res, xn_res)
            P.op("dve", TS(xn_ap, src_ap, rstd, None, ALU.mult), r=src_res + [rres], w=xn_res)
            b = tb[0] % 2
            tb[0] += 1
            pst = bank_bf(b)

            def tr():
                ins = None
                for j in range(nch):
                    ins = nc.tensor.transpose(out=pst[:, j * 128:(j + 1) * 128], in_=xn_ap[:, j * 128:(j + 1) * 128],
                                              identity=ident[:])
                return ins
            P.op("pe", tr, r=xn_res + ["ident"], w=[PB(b)])
            src3 = pst[:, 0:n].rearrange("p (j c) -> p j c", j=nch)
            dst3 = dstT[:, 0:nch, tok0:tok0 + 128]
            if evac_eng == "act":
                P.op("act", ACOPY(dst3, src3), w=[PB(b)] + dst_res)
            else:
                P.op("dve", VCOPY(dst3, src3), w=[PB(b)] + dst_res)

        def wview(name):
            return wb[name].rearrange("(c p) n -> p c n", p=128)

        out_keys = []

        def load_x(si, t):
            P.dma(DMA(xs[:, t, :], x_d[si, t * 128:(t + 1) * 128, :]), w=[("xs", t)], chan="xl%d" % t)

        def ffn(si, wg, wu, wd, final):
            A.reset()
            aT = A.bf(FC, 512)
            wgu = A.bf(2, 2, KC * 256).rearrange("p s g (k c) -> p s g k c", k=KC)
            wdt = A.bf(FC, DM)
            xn = A.bf(2, DM)
            sg = A.f32(2, 512)
            gfin = A.f32(DM) if final else None
            if final:
                P.dma(DMA(gfin, gfin_d.to_broadcast([128, DM])), w=["gfin"], chan="gf")
            wgv, wuv_, wdv_ = wview(wg), wview(wu), wview(wd)
            NP = FC // 2

            def load_gu(gi):
                s = gi % 2
                c0 = (gi % NP) * 256
                P.dma(DMA(wgu[:, s, 0, :, :], wgv[:, :, c0:c0 + 256]), r=[("W", wg)], w=[("wgu", s, 0)], chan="gu%d" % s)
                P.dma(DMA(wgu[:, s, 1, :, :], wuv_[:, :, c0:c0 + 256]), r=[("W", wu)], w=[("wgu", s, 1)], chan="gu%d" % s)

            load_gu(0)
            gi = 0
            for b in range(NB):
                for tt in range(4):
                    t = b * 4 + tt
                    norm_T(xs[:, t, :], DM, [("xs", t)], xn[:, tt % 2, :], [("xn", tt % 2)], hT, [("hT", t)], t * 128, "act")
                for h2 in range(2):
                    P.dma(DMA(wdt[:, h2 * 11:(h2 + 1) * 11, :], wdv_[:, h2 * 11:(h2 + 1) * 11, :]),
                          r=[("W", wd)], w=[("wdt", h2)], chan="wd%d" % h2)
                hres = [("hT", b * 4 + tt) for tt in range(4)]
                for fp in range(NP):
                    if gi + 1 < NB * NP:
                        load_gu(gi + 1)
                    s = gi % 2
                    for fi in range(2):
                        f = fp * 2 + fi
                        gb, ub = 2 + f % 2, 4 + f % 2
                        for g, dstb in ((0, gb), (1, ub)):
                            pairs = [(wgu[:, s, g, k, fi * 128:(fi + 1) * 128], hT[:, k, b * 512:(b + 1) * 512]) for k in range(KC)]
                            P.op("pe", MM(bank(dstb), pairs), r=hres + [("wgu", s, 0), ("wgu", s, 1)], w=[PB(dstb)])
                        P.op("act", ACT(sg[:, f % 2, :], bank(gb), AF.Silu), w=[PB(gb), ("sg", f % 2)])
                        P.op("dve", TT("dve", aT[:, f, :], bank(ub), sg[:, f % 2, :], ALU.mult), r=[("sg", f % 2)], w=[PB(ub), ("aT", f)])
                    gi += 1
                ares = [("aT", f) for f in range(FC)]
                for tt in range(4):
                    t = b * 4 + tt
                    for hf in range(2):
                        db = 6 + (tt * 2 + hf) % 2
                        pairs = [(aT[:, f, tt * 128:(tt + 1) * 128], wdt[:, f, hf * 512:(hf + 1) * 512]) for f in range(FC)]
                        P.op("pe", MM(bank(db), pairs), r=ares + [("wdt", 0), ("wdt", 1)], w=[PB(db)])
                        xsl = xs[:, t, hf * 512:(hf + 1) * 512]
                        P.op("dve", STT(xsl, bank(db), 0.5, xsl, ALU.mult, ALU.add), w=[PB(db), ("xs", t)])
                    if final:
                        rstd, rres = rms_rstd(xs[:, t, :], DM, xn[:, tt % 2, :], [("xs", t)], [("xn", tt % 2)])
                        P.op("dve", STT(xs[:, t, :], xs[:, t, :], rstd, gfin, ALU.mult, ALU.mult), r=[rres, "gfin"], w=[("xs", t)])
                        P.dma(DMA(y_d[si, t * 128:(t + 1) * 128, :], xs[:, t, :]), r=[("xs", t)], w=[("y", si, t)], chan="yst%d" % t)
                        out_keys.append(("y", si, t))
                        if si + 1 < nseq:
                            load_x(si + 1, t)
            P.barrier()

        def group_out(wrow0, mixedT, wo):
            P.op("dve", TS(rstd_g[:, 1, :], bank(6, 16), 1.0 / 512, EPS, ALU.mult, ALU.add), w=[PB(6), "rg1"])
            P.op("pool", TT("pool", rstd_g[:, 2, :], rstd_g[:, 1, :], nhalf[:, 0:16], ALU.pow), r=["rg1", "nhalf"], w=["rg2"])
            wv = wview("wout")
            c0 = wrow0 // 128
            for h2 in range(2):
                P.dma(DMA(wo[:, h2 * 2:(h2 + 1) * 2, :], wv[:, c0 + h2 * 2:c0 + (h2 + 1) * 2, :]),
                      r=[("W", "wout")], w=[("wo", h2)], chan="wo%d" % h2)
            mres = [("mx", j, b) for j in range(4) for b in range(NB)]
            for t in range(NT):
                for hf in range(2):
                    db = (t * 2 + hf) % 2
                    pairs = [(mixedT[:, j, t * 128:(t + 1) * 128], wo[:, j, hf * 512:(hf + 1) * 512]) for j in range(4)]
                    P.op("pe", MM(bank(db), pairs), r=mres + [("wo", 0), ("wo", 1)], w=[PB(db)])
                    xsl = xs[:, t, hf * 512:(hf + 1) * 512]
                    P.op("dve", STT(xsl, bank(db), rstd_g[:, 2, t:t + 1], xsl, ALU.mult, ALU.add), r=["rg2"], w=[PB(db), ("xs", t)])

        def sumsq(mixedT, h, b, first, sq):
            r0 = 64 * (h % 2)
            j = h // 2
            msl = mixedT[r0:r0 + 64, j, b * 512:(b + 1) * 512]
            P.op("pool", TT("pool", sq[r0:r0 + 64, b % 2, :], msl, msl, ALU.mult), r=[("mx", j, b)], w=[("sq", b % 2)])
            items = []
            for tt in range(4):
                t = b * 4 + tt
                items.append((ps[:, 6 * 512 + t:6 * 512 + t + 1], sq[r0:r0 + 64, b % 2, tt * 128:(tt + 1) * 128],
                              ones_b[r0:r0 + 64, 0:1], bool(first and tt == 0), False))
            P.op("pe", MMS(items), r=[("sq", b % 2), "ones_b"], w=[PB(6)])

        def mixer(si):
            A.reset()
            mixedT = A.bf(4, SEQ)
            wo = A.bf(4, DM)
            PT = A.bf(3, 1024)
            tmp = A.f32(2, 512)
            rden = A.f32(2, 512)
            sq = A.bf(2, 512)
            base = A.off
            tabm = A.f32(SEQ)
            cqT = A.bf(3, SEQ)
            ckvT = A.bf(2, SEQ)
            kT = A.bf(SEQ)
            qT = A.bf(SEQ)
            vaug = A.bf(NT, 128)
            wuq = A.bf(3, 1024)
            wuk = A.bf(2, 512)
            wuv = A.bf(2, 512)
            wlat = A.bf(KC, 704)
            P.dma(DMA(tabm, tabm_d[:, :]), w=["tabm"], chan="tb")
            P.dma(DMA(wlat, wview("wlat")), r=[("W", "wlat")], w=["wlat"], chan="wl")
            P.dma(DMA(wuq, wview("wuq")), r=[("W", "wuq")], w=["wuq"], chan="wq")
            P.dma(DMA(wuk, wview("wuk")), r=[("W", "wuk")], w=["wuk"], chan="wk")
            P.dma(DMA(wuv, wview("wuv")), r=[("W", "wuv")], w=["wuv"], chan="wv")
            P.op("pool", (lambda: nc.gpsimd.memset(vaug[:, :, 64:128], 1.0)), w=["vaug1"])
            for t in range(NT):
                norm_T(xs[:, t, :], DM, [("xs", t)], xn_big[:, t % 2, :], [("xnb", t % 2)], hT, [("hT", t)], t * 128, "act")
            for t in range(NT):
                qb_, kb_ = 2 + t % 2, 4 + t % 2
                pq = [(hT[:, k, t * 128:(t + 1) * 128], wlat[:, k, 0:384]) for k in range(KC)]
                pk = [(hT[:, k, t * 128:(t + 1) * 128], wlat[:, k, 384:640]) for k in range(KC)]
                P.op("pe", MM(bank(qb_, 384), pq), r=[("hT", t), "wlat"], w=[PB(qb_)])
                P.op("pe", MM(bank(kb_, 256), pk), r=[("hT", t), "wlat"], w=[PB(kb_)])
                norm_T(bank(qb_, 384), 384, [PB(qb_)], xn_big[:, 0, 0:384], [("xnb", 0)], cqT, [("cqT", t // 4)], t * 128, "dve")
                norm_T(bank(kb_, 256), 256, [PB(kb_)], xn_big[:, 1, 0:256], [("xnb", 1)], ckvT, [("ckvT", t // 4)], t * 128, "dve")
            for b in range(NB):
                kb_ = 6 + b % 2
                pr = [(wlat[:, k, 640:704], hT[:, k, b * 512:(b + 1) * 512]) for k in range(KC)]
                P.op("pe", MM(bank(kb_, 512, 64), pr), r=[("hT", 4 * b + i) for i in range(4)] + ["wlat"], w=[PB(kb_)])
                P.op("dve", TT("dve", tmp[0:32, b % 2, :], bank(kb_, 512, 32), tabm[0:32, b * 512:(b + 1) * 512], ALU.mult),
                     r=["tabm"], w=[PB(kb_), ("tmp", b % 2)])
                P.op("dve", TT("dve", rden[0:32, b % 2, :], bank(kb_, 512, 32, 32), tabm[32:64, b * 512:(b + 1) * 512], ALU.mult),
                     r=["tabm"], w=[PB(kb_), ("rden", b % 2)])
                P.op("pool", TT("pool", kT[64:96, b * 512:(b + 1) * 512], tmp[0:32, b % 2, :], rden[0:32, b % 2, :], ALU.add),
                     r=[("tmp", b % 2), ("rden", b % 2)], w=[("kTr", b)])
            P.barrier()
            scale = 96 ** -0.5
            stage = 0
            for h in range(8):
                for b in range(NB):
                    qb_, kb_ = 0 + b % 2, 2 + b % 2
                    bs = slice(b * 512, (b + 1) * 512)
                    pq = [(wuq[:, k, h * 128:(h + 1) * 128], cqT[:, k, bs]) for k in range(3)]
                    P.op("pe", MM(bank(qb_), pq), r=[("cqT", b), "wuq"], w=[PB(qb_)])
                    P.op("act", ACOPY(qT[0:64, bs], bank(qb_, 512, 64)), w=[PB(qb_), ("qTn", b)])
                    P.op("dve", TT("dve", tmp[64:96, b % 2, :], bank(qb_, 512, 32, 64), tabm[64:96, bs], ALU.mult),
                         r=["tabm"], w=[PB(qb_), ("tmp", b % 2)])
                    P.op("dve", TT("dve", rden[64:96, b % 2, :], bank(qb_, 512, 32, 96), tabm[96:128, bs], ALU.mult),
                         r=["tabm"], w=[PB(qb_), ("rden", b % 2)])
                    P.op("pool", TT("pool", qT[64:96, bs], tmp[64:96, b % 2, :], rden[64:96, b % 2, :], ALU.add),
                         r=[("tmp", b % 2), ("rden", b % 2)], w=[("qTr", b)])
                    pk = [(wuk[:, k, h * 64:(h + 1) * 64], ckvT[:, k, bs]) for k in range(2)]
                    P.op("pe", MM(bank(kb_, 512, 64), pk), r=[("ckvT", b), "wuk"], w=[PB(kb_)])
                    P.op("act", ACOPY(kT[0:64, bs], bank(kb_, 512, 64)), w=[PB(kb_), ("kTn", b)])
                for g in range(2):
                    vb_ = 4 + g
                    items = []
                    for tt in range(8):
                        t = g * 8 + tt
                        for k in range(2):
                            items.append((ps[:, vb_ * 512 + tt * 64:vb_ * 512 + (tt + 1) * 64], ckvT[:, k, t * 128:(t + 1) * 128],
                                          wuv[:, k, h * 64:(h + 1) * 64], bool(tt == 0 and k == 0), bool(k == 1)))
                    P.op("pe", MMS(items), r=[("ckvT", g * 2), ("ckvT", g * 2 + 1), "wuv"], w=[PB(vb_)])
                    P.op("dve", VCOPY(vaug[:, g * 8:(g + 1) * 8, 0:64], bank(vb_).rearrange("p (t c) -> p t c", t=8)),
                         w=[PB(vb_), ("vaug", g)])
                kres = [("kTn", b) for b in range(NB)] + [("kTr", b) for b in range(NB)]
                vres = [("vaug", 0), ("vaug", 1), "vaug1"]
                for qb in range(NB):
                    ob = 4 + qb % 2
                    qs = slice(qb * 512, (qb + 1) * 512)
                    for cp in range(8):
                        sbk = (stage % 2) * 2
                        st = stage % 3
                        stage += 1
                        items = [(bank(sbk + i), kT[0:96, (cp * 2 + i) * 128:(cp * 2 + i + 1) * 128], qT[0:96, qs], True, True)
                                 for i in range(2)]
                        P.op("pe", MMS(items), r=kres + [("qTn", qb), ("qTr", qb)], w=[PB(sbk), PB(sbk + 1)])
                        P.op("act", ACT(PT[:, st, :], ps[:, sbk * 512:(sbk + 2) * 512], AF.Exp, scale=scale),
                             w=[PB(sbk), PB(sbk + 1), ("PT", st)])
                        items = []
                        for i in range(2):
                            c = cp * 2 + i
                            items.append((bank(ob), vaug[:, c, :], PT[:, st, i * 512:(i + 1) * 512], bool(c == 0), bool(c == NT - 1)))
                        P.op("pe", MMS(items), r=vres + [("PT", st)], w=[PB(ob)])
                    r0 = 64 * (h % 2)
                    P.op("dve", RECIP(rden[64:128, qb % 2, :], bank(ob, 512, 64, 64)), w=[PB(ob), ("rden", qb % 2)])
                    P.op("dve", TT("dve", mixedT[r0:r0 + 64, h // 2, qs], bank(ob, 512, 64), rden[64:128, qb % 2, :], ALU.mult),
                         r=[("rden", qb % 2)], w=[PB(ob), ("mx", h // 2, qb)])
                    sumsq(mixedT, h, qb, first=(h == 0 and qb == 0), sq=sq)
            group_out(0, mixedT, wo)
            P.barrier()
            A.reset(base)
            tabd = A.f32(SEQ)
            qdT = A.bf(SEQ)
            kdT = A.bf(SEQ)
            acc = A.f32(SEQ)
            wdqk = A.bf(KC, 160)
            wdv = A.bf(KC, 256)
            vd = A.bf(3 * NT * 4, 65)
            P.dma(DMA(tabd, tabd_d[:, :]), w=["tabd"], chan="tb")
            P.op("pool", (lambda: nc.gpsimd.memset(vd[:, :, 64:65], 1.0)), w=["vd1"])
            dscale = 0.125
            stage = 0
            mcnt = 0
            allh = [("hT", i) for i in range(NT)]
            for h in range(8):
                g, hh = h // 4, h % 4
                if hh == 0:
                    P.dma(DMA(wdv, wview("wdv")[:, :, g * 256:(g + 1) * 256]), r=[("W", "wdv")], w=["wdv"], chan="dv")
                    vcnt = 0
                    for o, d in enumerate((1, 4, 16)):
                        L = SEQ // d
                        for r in range(d):
                            for c in range(L // 128):
                                tl = (r * L) // 128 + c
                                vb_ = 5 if vcnt % 2 else 7
                                vcnt += 1
                                pairs = []
                                for k in range(KC):
                                    lh = hT[:, k, :].rearrange("p (a b) -> p a b", b=d)[:, 128 * c:128 * c + 128, r]
                                    pairs.append((lh, wdv[:, k, :]))
                                P.op("pe", MM(bank(vb_, 256), pairs), r=allh + ["wdv"], w=[PB(vb_)])
                                src = bank(vb_, 256).rearrange("p (h c) -> p h c", h=4)
                                i0 = (o * NT + tl) * 4
                                if vcnt % 2:
                                    P.op("act", ACOPY(vd[:, i0:i0 + 4, 0:64], src), w=[PB(vb_), ("vd", o)])
                                else:
                                    P.op("dve", VCOPY(vd[:, i0:i0 + 4, 0:64], src), w=[PB(vb_), ("vd", o)])
                P.dma(DMA(wdqk, wview("wdqk")[:, :, h * 160:(h + 1) * 160]), r=[("W", "wdqk")], w=["wdqk"], chan="dq")
                for which, dst, nm in ((0, qdT, "qd"), (1, kdT, "kd")):
                    for b in range(NB):
                        pb_ = 4 if (b + which) % 2 == 0 else 7
                        bs = slice(b * 512, (b + 1) * 512)
                        pairs = [(wdqk[:, k, which * 80:(which + 1) * 80], hT[:, k, bs]) for k in range(KC)]
                        P.op("pe", MM(bank(pb_, 512, 80), pairs), r=[("hT", 4 * b + i) for i in range(4)] + ["wdqk"], w=[PB(pb_)])
                        P.op("act", ACOPY(dst[0:64, bs], bank(pb_, 512, 64)), w=[PB(pb_), (nm, b)])
                        P.op("dve", TT("dve", tmp[0:16, b % 2, :], bank(pb_, 512, 16), tabd[0:16, bs], ALU.mult),
                             r=["tabd"], w=[PB(pb_), ("tmp", b % 2)])
                        P.op("dve", TT("dve", rden[0:16, b % 2, :], bank(pb_, 512, 16, 64), tabd[64:80, bs], ALU.mult),
                             r=["tabd"], w=[PB(pb_), ("rden", b % 2)])
                        P.op("pool", TT("pool", dst[0:16, bs], tmp[0:16, b % 2, :], rden[0:16, b % 2, :], ALU.add),
                             r=[("tmp", b % 2), ("rden", b % 2)], w=[(nm, b)])
                qkres = [("qd", b) for b in range(NB)] + [("kd", b) for b in range(NB)]
                for o, d in enumerate((1, 4, 16)):
                    L = SEQ // d
                    nch = L // 128
                    started = set()
                    kv = kdT[0:64, :].rearrange("p (a b) -> p a b", b=d)
                    qv = qdT[0:64, :].rearrange("p (a b) -> p a b", b=d)
                    for r in range(d):
                        for c in range(nch):
                            lo, hi = max(0, 128 * c - 64), min(L, 128 * c + 192)
                            n = hi - lo
                            mlo = lo - (128 * c - 64)
                            st = stage % 3
                            sbk = stage % 3
                            stage += 1
                            P.op("pe", MM(bank(sbk, n), [(kv[:, 128 * c:128 * c + 128, r], qv[:, lo:hi, r])]), r=qkres, w=[PB(sbk)])
                            P.op("act", ACT(PT[:, st, 512:512 + n], bank(sbk, n), AF.Exp, scale=dscale), w=[PB(sbk), ("PTe", st)])
                            me = "pool" if mcnt % 2 else "dve"
                            mcnt += 1
                            P.op(me, TT(me, PT[:, st, 0:n], PT[:, st, 512:512 + n], mask[:, mlo:mlo + n], ALU.mult),
                                 r=[("PTe", st), "mask"], w=[("PT", st)])
                            tl = (r * L) // 128 + c
                            p_lo = r * L + lo
                            a = p_lo
                            while a < p_lo + n:
                                e = min(p_lo + n, (a // 512 + 1) * 512)
                                bk = a // 512
                                ob = 3 + bk % 2
                                first = (o, bk) not in started
                                started.add((o, bk))
                                i0 = (o * NT + tl) * 4 + hh
                                P.op("pe", MMS([(ps[0:65, ob * 512 + a % 512:ob * 512 + a % 512 + (e - a)], vd[:, i0, :],
                                                 PT[:, st, a - p_lo:e - p_lo], bool(first), False)]),
                                     r=[("PT", st), ("vd", o), "vd1"], w=[PB(ob)])
                                a = e
                            pend = r * L + min(L, 128 * c + 192)
                            done = []
                            if c == nch - 1:
                                if pend % 512 == 0:
                                    done.append(pend // 512 - 1)
                            elif d == 1 and c % 4 == 0 and c > 0:
                                done.append(c // 4 - 1)
                            for bk in done:
                                ob = 3 + bk % 2
                                if d == 1:
                                    P.op("dve", VCOPY(acc[0:65, bk * 512:(bk + 1) * 512], bank(ob, 512, 65)), w=[PB(ob), "acc"])
                                elif d == 4:
                                    dstv = acc[0:65, :].rearrange("p (a b) -> p a b", b=4)[:, :, bk]
                                    P.op("dve", TT("dve", dstv, bank(ob, 512, 65), dstv, ALU.add), w=[PB(ob), "acc"])
                                else:
                                    dstv = acc[0:65, :].rearrange("p (a b) -> p b a", b=16)[:, bk * 4:(bk + 1) * 4, :]
                                    srcv = bank(ob, 512, 65).rearrange("p (r a) -> p r a", r=4)
                                    P.op("dve", TT("dve", dstv, srcv, dstv, ALU.add), w=[PB(ob), "acc"])
                accd = acc[64:65, :].rearrange("p (a b) -> p a b", a=4)
                P.op("dve", RECIP(rden[64:65, :, :], accd[:, 0:2, :]), r=["acc"], w=[("rden", 0), ("rden", 1)])
                P.op("dve", RECIP(tmp[64:65, :, :], accd[:, 2:4, :]), r=["acc"], w=[("tmp", 0), ("tmp", 1)])
                r0 = 64 * (h % 2)
                for b in range(NB):
                    rsrc = (rden if b < 2 else tmp)[64:65, b % 2, :]
                    rres = ("rden" if b < 2 else "tmp", b % 2)
                    bb = 5 if b % 2 == 0 else 7
                    bs = slice(b * 512, (b + 1) * 512)
                    P.op("pe", MM(bank(bb, 512, 64), [(ones_f[64:65, 0:64], rsrc)]), r=[rres, "ones_f"], w=[PB(bb)])
                    P.op("dve", TT("dve", mixedT[r0:r0 + 64, h // 2, bs], acc[0:64, bs], bank(bb, 512, 64), ALU.mult),
                         r=["acc"], w=[PB(bb), ("mx", h // 2, b)])
                    sumsq(mixedT, h, b, first=(h == 0 and b == 0), sq=sq)
            group_out(512, mixedT, wo)
            P.barrier()

        xn_big = sb("xn_big", [128, 2, DM], BF16)

        for t in range(NT):
            load_x(0, t)
        for si in range(nseq):
            ffn(si, "wg1", "wu1", "wd1", final=False)
            if dbg_stage != "ffn":
                mixer(si)
            ffn(si, "wg2", "wu2", "wd2", final=True)
        P.op("sp", lambda: nc.sync.nop(), r=list(out_keys))
        P.emit(stack)
        print("PROG ops", P.stats[0], "sig", P.stats[1], "arena peak", A.peak, flush=True)
    return nc


def _rope_tables():
    pos = np.arange(SEQ, dtype=np.float32)

    def cs(rot):
        half = rot // 2
        inv = np.float32(ROPE_THETA) ** (-np.arange(half, dtype=np.float32) * np.float32(2.0 / rot))
        ang = pos[None, :] * inv.astype(np.float32)[:, None]
        return np.cos(ang).astype(np.float32), np.sin(ang).astype(np.float32)

    cm, sm = cs(32)
    cd, sd = cs(16)
    tabm = np.ones((128, SEQ), np.float32)
    for base in (0, 64):
        tabm[base:base + 16] = cm
        tabm[base + 16:base + 32] = cm
        tabm[base + 32:base + 48] = -sm
        tabm[base + 48:base + 64] = sm
    tabd = np.ones((128, SEQ), np.float32)
    tabd[0:8] = cd
    tabd[8:16] = cd
    tabd[64:72] = -sd
    tabd[72:80] = sd
    return tabm, tabd


def _host_layout(inp):
    g = lambda k: np.asarray(inp[k], np.float32)
    w_in = g("w_in")[0]
    sw32 = np.r_[16:32, 0:16]
    kr = w_in[:, 640:672]
    wlat = np.concatenate([w_in[:, 0:640], kr, kr[:, sw32]], axis=1)
    wuq_src = g("mla_w_uq")[0].reshape(384, 8, 96)
    wuq = np.concatenate([wuq_src, wuq_src[:, :, 64:96][:, :, sw32]], axis=2).reshape(384, 1024)
    wukv = g("mla_w_ukv")[0].reshape(256, 8, 128)
    wuk = wukv[:, :, 0:64].reshape(256, 512)
    wuv = wukv[:, :, 64:128].reshape(256, 512)
    qkv = w_in[:, 672:].reshape(DM, 3, 8, 64)
    sw16 = np.r_[8:16, 0:8]
    qd, kd = qkv[:, 0], qkv[:, 1]
    wdqk = np.concatenate([qd, qd[:, :, sw16], kd, kd[:, :, sw16]], axis=2).reshape(DM, 1280)
    wdv = qkv[:, 2].reshape(DM, 512)

    def fm(v):
        v = np.asarray(v, np.float32).reshape(-1)
        return v.reshape(-1, 128).T

    gains = np.concatenate([fm(g("ffn1_norm")), fm(g("mix_norm")), fm(g("mla_q_norm")), fm(g("mla_kv_norm")),
                            fm(g("mla_out_norm")), fm(g("dil_out_norm")), fm(g("ffn2_norm"))], axis=1)
    assert gains.shape == (128, NG)
    tabm, tabd = _rope_tables()
    jj = np.arange(128)[:, None]
    ii = np.arange(256)[None, :]
    mask = ((ii - jj >= 0) & (ii - jj <= 128)).astype(np.float32).astype(ml_dtypes.bfloat16)
    ident = np.eye(128, dtype=np.float32).astype(ml_dtypes.bfloat16)
    c = np.ascontiguousarray
    shared = {
        "wg1": c(g("ffn1_w_gate")[0]), "wu1": c(g("ffn1_w_up")[0]), "wd1": c(g("ffn1_w_down")[0]),
        "wg2": c(g("ffn2_w_gate")[0]), "wu2": c(g("ffn2_w_up")[0]), "wd2": c(g("ffn2_w_down")[0]),
        "wlat": c(wlat), "wuq": c(wuq), "wuk": c(wuk), "wuv": c(wuv), "wdqk": c(wdqk), "wdv": c(wdv),
        "wout": c(g("w_out")[0]), "gains": c(gains), "gfin": c(g("final_norm").reshape(1, DM)),
        "tabm": tabm, "tabd": tabd, "mask": c(mask), "ident": c(ident),
    }
    return shared


def kernel(**inputs):
    x = np.asarray(inputs["x"], np.float32)
    B = x.shape[0]
    nseq = B // N_CORES
    shared = _host_layout(inputs)
    nc = build_program(nseq)
    in_maps = []
    for c in range(N_CORES):
        m = dict(shared)
        m["x"] = np.ascontiguousarray(x[c * nseq:(c + 1) * nseq])
        in_maps.append(m)
    res = run_bass_kernel_spmd(nc, in_maps, core_ids=list(range(N_CORES)))
    out = np.concatenate([np.asarray(r["y"], np.float32) for r in res.results], axis=0)
    return out
```

```python
import math
from contextlib import ExitStack

import numpy as np
import ml_dtypes

import concourse.bass as bass
import concourse.mybir as mybir
from concourse.bass_utils import run_bass_kernel_spmd

F32 = mybir.dt.float32
BF16 = mybir.dt.bfloat16
AF = mybir.ActivationFunctionType
ALU = mybir.AluOpType

N_CORES = 8
SEQ = 2048
DM = 1024
DFF = 2816
NT = SEQ // 128
NB = SEQ // 512
KC = DM // 128
FC = DFF // 128
EPS = 1e-6
ROPE_THETA = 500000.0

G1, GMIX, GQ, GKV, GMO, GDO, G2 = 0, 8, 16, 19, 21, 25, 29
NG = 37


class Prog:
    ENGS = ("pe", "act", "dve", "pool", "sp")

    def __init__(self, nc):
        self.nc = nc
        self.E = {"pe": nc.tensor, "act": nc.scalar, "dve": nc.vector, "pool": nc.gpsimd, "sp": nc.sync}
        self.ops = []
        self.last_w = {}
        self.readers = {}
        self.bar = set()
        self.last_eng = {}
        self.last_chan = {}

    def _add(self, eng, fn, r, w, chan=None):
        i = len(self.ops)
        deps = set(self.bar)
        for k in r:
            if k in self.last_w:
                deps.add(self.last_w[k])
        for k in w:
            if k in self.last_w:
                deps.add(self.last_w[k])
            deps.update(self.readers.get(k, ()))
        for k in r:
            self.readers.setdefault(k, []).append(i)
        for k in w:
            self.last_w[k] = i
            self.readers[k] = []
        self.ops.append([eng, fn, deps, chan, None])
        if chan is None:
            self.last_eng[eng] = i
        else:
            self.last_chan[chan] = i
        return i

    def op(self, eng, fn, r=(), w=()):
        return self._add(eng, fn, r, w)

    def dma(self, fn, r=(), w=(), chan=None, q="sp"):
        assert chan is not None
        return self._add(q, fn, r, w, chan)

    def barrier(self):
        self.bar = set(self.last_eng.values()) | set(self.last_chan.values())

    def emit(self, stack):
        nc = self.nc
        ops = self.ops
        has_dep = [False] * len(ops)
        for o in ops:
            for d in o[2]:
                has_dep[d] = True
        cnt = {e: 0 for e in self.ENGS}
        ccnt = {}
        for i, o in enumerate(ops):
            if o[3] is not None:
                ccnt[o[3]] = ccnt.get(o[3], 0) + 16
                o[4] = (("c", o[3]), ccnt[o[3]])
            elif has_dep[i]:
                cnt[o[0]] += 1
                o[4] = (("e", o[0]), cnt[o[0]])
        sems = {}
        for e in self.ENGS:
            if cnt[e]:
                sems[("e", e)] = stack.enter_context(nc.semaphore("s_" + e))
        for c in ccnt:
            sems[("c", c)] = stack.enter_context(nc.semaphore("c_" + c))
        waited = {e: {} for e in self.ENGS}
        for i, o in enumerate(ops):
            eng, fn, deps, chan, sig = o
            need = {}
            for d in deps:
                od = ops[d]
                if od[3] is None and od[0] == eng and eng in ("pe", "sp"):
                    continue
                s, v = od[4]
                if need.get(s, 0) < v:
                    need[s] = v
            E = self.E[eng]
            for s, v in need.items():
                if waited[eng].get(s, 0) >= v:
                    continue
                waited[eng][s] = v
                E.wait_ge(sems[s], v)
            ins = fn()
            if sig is not None:
                ins.then_inc(sems[sig[0]], 16 if chan is not None else 1)
        self.stats = (len(ops), dict(cnt), dict(ccnt))


class Arena:
    def __init__(self, t, n):
        self.t, self.n, self.off = t, n, 0
        self.peak = 0

    def reset(self, off=0):
        self.off = off

    def _take(self, n):
        a = self.off
        self.off += n
        self.peak = max(self.peak, self.off)
        assert self.off <= self.n, ("arena overflow", self.off, self.n)
        return a

    def bf(self, *shape):
        n = int(np.prod(shape))
        a = self._take(n)
        ap = self.t[:, a:a + n]
        if len(shape) == 2:
            return ap.rearrange("p (a b) -> p a b", a=shape[0])
        if len(shape) == 3:
            return ap.rearrange("p (a b c) -> p a b c", a=shape[0], b=shape[1])
        return ap

    def f32(self, *shape):
        self.off = (self.off + 1) // 2 * 2
        n = int(np.prod(shape))
        a = self._take(2 * n)
        ap = self.t[:, a:a + 2 * n].bitcast(F32)
        if len(shape) == 2:
            return ap.rearrange("p (a b) -> p a b", a=shape[0])
        return ap


def build_program(nseq, dbg_stage=None):
    nc = bass.Bass("TRN2", target_bir_lowering=False)
    P = Prog(nc)

    def din(name, shape, dt=F32):
        return nc.dram_tensor(name, list(shape), dt, kind="ExternalInput").ap()

    def dscr(name, shape):
        return nc.dram_tensor(name, list(shape), BF16, kind="Internal").ap()

    x_d = din("x", [nseq, SEQ, DM])
    y_d = nc.dram_tensor("y", [nseq, SEQ, DM], F32, kind="ExternalOutput").ap()
    wsrc = {
        "wg1": din("wg1", [DM, DFF]), "wu1": din("wu1", [DM, DFF]), "wd1": din("wd1", [DFF, DM]),
        "wg2": din("wg2", [DM, DFF]), "wu2": din("wu2", [DM, DFF]), "wd2": din("wd2", [DFF, DM]),
        "wlat": din("wlat", [DM, 704]), "wuq": din("wuq", [384, 1024]),
        "wuk": din("wuk", [256, 512]), "wuv": din("wuv", [256, 512]),
        "wdqk": din("wdqk", [DM, 1280]), "wdv": din("wdv", [DM, 512]), "wout": din("wout", [DM, DM]),
    }
    wgain = {"wg1": G1, "wu1": G1, "wd1": None, "wg2": G2, "wu2": G2, "wd2": None, "wlat": GMIX, "wuq": GQ,
             "wuk": GKV, "wuv": GKV, "wdqk": GMIX, "wdv": GMIX, "wout": GMO}
    wb = {k: dscr(k + "_b", v.shape) for k, v in wsrc.items()}
    gains_d = din("gains", [128, NG])
    gfin_d = din("gfin", [1, DM])
    tabm_d = din("tabm", [128, SEQ])
    tabd_d = din("tabd", [128, SEQ])
    mask_d = din("mask", [128, 512], BF16)
    ident_d = din("ident", [128, 128], BF16)

    stack = ExitStack()
    with stack:
        sb = lambda name, shape, dt: stack.enter_context(nc.sbuf_tensor("sb_" + name, shape, dt))
        xs = sb("xs", [128, NT, DM], F32)
        hT = sb("hT", [128, KC, SEQ], BF16)
        ident = sb("ident", [128, 128], BF16)
        ones_b = sb("ones_b", [128, 8], BF16)
        ones_f = sb("ones_f", [128, 64], F32)
        nhalf = sb("nhalf", [128, 16], F32)
        gains = sb("gains", [128, NG], F32)
        mask = sb("mask", [128, 512], BF16)
        stat = sb("stat", [128, 3, 16], F32)
        rstd_g = sb("rstd_g", [128, 3, 16], F32)
        ARENA_N = 51800
        arena_t = sb("arena", [128, ARENA_N], BF16)
        ps = stack.enter_context(nc.psum_tensor("ps", [128, 8 * 512], F32))
        A = Arena(arena_t, ARENA_N)

        def bank(b, n=512, rows=128, r0=0):
            return ps[r0:r0 + rows, b * 512:b * 512 + n]

        def bank_bf(b, n=1024):
            return ps[:, b * 512:(b + 1) * 512].bitcast(BF16)[:, 0:n]

        PB = lambda b: ("ps", b)

        P.dma(lambda: nc.sync.dma_start(out=ident[:], in_=ident_d[:, :]), w=["ident"], chan="k0")
        P.dma(lambda: nc.sync.dma_start(out=gains[:], in_=gains_d[:, :]), w=["gains"], chan="k1")
        P.dma(lambda: nc.sync.dma_start(out=mask[:], in_=mask_d[:, :]), w=["mask"], chan="k2")
        P.op("pool", lambda: nc.gpsimd.memset(ones_b[:], 1.0), w=["ones_b"])
        P.op("pool", lambda: nc.gpsimd.memset(ones_f[:], 1.0), w=["ones_f"])
        P.op("pool", lambda: nc.gpsimd.memset(nhalf[:], -0.5), w=["nhalf"])

        for t in range(NT):
            P.dma((lambda t=t: nc.sync.dma_start(out=xs[:, t, :], in_=x_d[0, t * 128:(t + 1) * 128, :])), w=[("xs", t)], chan="xl%d" % t)
        A.reset()
        stg_f = A.f32(4, DFF)
        stg_b = A.bf(4, DFF)
        it = 0
        for name, src in wsrc.items():
            K, N = src.shape
            for j in range(K // 128):
                s = it % 4
                gcol = wgain[name]
                if name == "wout" and j >= 4:
                    gcol = GDO - 4
                P.dma(lambda s=s, src=src, j=j, N=N: nc.sync.dma_start(out=stg_f[:, s, 0:N], in_=src[j * 128:(j + 1) * 128, :]),
                      w=[("stf", s)], chan="il%d" % s)
                if gcol is None:
                    if it % 2 == 0:
                        P.op("act", lambda s=s, N=N: nc.scalar.copy(out=stg_b[:, s, 0:N], in_=stg_f[:, s, 0:N]),
                             r=[("stf", s)], w=[("stb", s)])
                    else:
                        P.op("dve", lambda s=s, N=N: nc.vector.tensor_copy(out=stg_b[:, s, 0:N], in_=stg_f[:, s, 0:N]),
                             r=[("stf", s)], w=[("stb", s)])
                else:
                    gc = gcol + j
                    if it % 2 == 0:
                        P.op("act", lambda s=s, N=N, gc=gc: nc.scalar.activation(out=stg_b[:, s, 0:N], in_=stg_f[:, s, 0:N],
                                                                                 func=AF.Copy, scale=gains[:, gc:gc + 1]),
                             r=[("stf", s), "gains"], w=[("stb", s)])
                    else:
                        P.op("dve", lambda s=s, N=N, gc=gc: nc.vector.tensor_scalar(out=stg_b[:, s, 0:N], in0=stg_f[:, s, 0:N],
                                                                                    scalar1=gains[:, gc:gc + 1], scalar2=None,
                                                                                    op0=ALU.mult),
                             r=[("stf", s), "gains"], w=[("stb", s)])
                P.dma(lambda s=s, name=name, j=j, N=N: nc.sync.dma_start(out=wb[name][j * 128:(j + 1) * 128, :], in_=stg_b[:, s, 0:N]),
                      r=[("stb", s)], w=[("W", name)], chan="is%d" % s)
                it += 1
        P.barrier()

        def MM(out, pairs, start=True, skip=False):
            def fn():
                ins = None
                n = len(pairs)
                for i, (l, r) in enumerate(pairs):
                    ins = nc.tensor.matmul(out, lhsT=l, rhs=r, start=(start and i == 0), stop=(i == n - 1),
                                           skip_group_check=skip)
                return ins
            return fn

        def MMS(items):
            def fn():
                ins = None
                for (o, l, r, st, sp) in items:
                    ins = nc.tensor.matmul(o, lhsT=l, rhs=r, start=st, stop=sp, skip_group_check=True)
                return ins
            return fn

        def ACT(out, in_, func, **kw):
            return lambda: nc.scalar.activation(out=out, in_=in_, func=func, **kw)

        def ACOPY(out, in_):
            return lambda: nc.scalar.copy(out=out, in_=in_)

        def TT(eng, out, in0, in1, op):
            e = nc.vector if eng == "dve" else nc.gpsimd
            return lambda: e.tensor_tensor(out=out, in0=in0, in1=in1, op=op)

        def TS(out, in0, s1, s2, op0, op1=None):
            if op1 is None:
                return lambda: nc.vector.tensor_scalar(out=out, in0=in0, scalar1=s1, scalar2=None, op0=op0)
            return lambda: nc.vector.tensor_scalar(out=out, in0=in0, scalar1=s1, scalar2=s2, op0=op0, op1=op1)

        def STT(out, in0, scalar, in1, op0, op1):
            return lambda: nc.vector.scalar_tensor_tensor(out=out, in0=in0, scalar=scalar, in1=in1, op0=op0, op1=op1)

        def VCOPY(out, in_):
            return lambda: nc.vector.tensor_copy(out=out, in_=in_)

        def RECIP(out, in_):
            return lambda: nc.vector.reciprocal(out=out, in_=in_)

        def DMA(out, in_):
            return lambda: nc.sync.dma_start(out=out, in_=in_)

        statc = [0]

        def rms_rstd(src_ap, n, junk_ap, src_res, junk_res):
            c = statc[0] % 16
            statc[0] += 1
            P.op("act", ACT(junk_ap, src_ap, AF.Square, accum_out=stat[:, 0, c:c + 1]), r=src_res, w=junk_res + [("st0", c)])
            P.op("dve", TS(stat[:, 1, c:c + 1], stat[:, 0, c:c + 1], 1.0 / n, EPS, ALU.mult, ALU.add), r=[("st0", c)], w=[("st1", c)])
            P.op("pool", TT("pool", stat[:, 2, c:c + 1], stat[:, 1, c:c + 1], nhalf[:, 0:1], ALU.pow),
                 r=[("st1", c), "nhalf"], w=[("st2", c)])
            return stat[:, 2, c:c + 1], ("st2", c)

        tb = [0]

        def norm_T(src_ap, n, src_res, xn_ap, xn_res, dstT, dst_res, tok0, evac_eng):
            nch = n // 128
            rstd, rres = rms_rstd(src_ap, n, xn_ap, src_res, xn_res)
            P.op("dve", TS(xn_ap, src_ap, rstd, None, ALU.mult), r=src_res + [rres], w=xn_res)
            b = tb[0] % 2
            tb[0] += 1
            pst = bank_bf(b)

            def tr():
                ins = None
                for j in range(nch):
                    ins = nc.tensor.transpose(out=pst[:, j * 128:(j + 1) * 128], in_=xn_ap[:, j * 128:(j + 1) * 128],
                                              identity=ident[:])
                return ins
            P.op("pe", tr, r=xn_res + ["ident"], w=[PB(b)])
            src3 = pst[:, 0:n].rearrange("p (j c) -> p j c", j=nch)
            dst3 = dstT[:, 0:nch, tok0:tok0 + 128]
            if evac_eng == "act":
                P.op("act", ACOPY(dst3, src3), w=[PB(b)] + dst_res)
            else:
                P.op("dve", VCOPY(dst3, src3), w=[PB(b)] + dst_res)

        def wview(name):
            return wb[name].rearrange("(c p) n -> p c n", p=128)

        out_keys = []

        def load_x(si, t):
            P.dma(DMA(xs[:, t, :], x_d[si, t * 128:(t + 1) * 128, :]), w=[("xs", t)], chan="xl%d" % t)

        def ffn(si, wg, wu, wd, final):
            A.reset()
            aT = A.bf(FC, 512)
            wgu = A.bf(2, 2, KC * 256).rearrange("p s g (k c) -> p s g k c", k=KC)
            wdt = A.bf(FC, DM)
            xn = A.bf(2, DM)
            sg = A.f32(2, 512)
            gfin = A.f32(DM) if final else None
            if final:
                P.dma(DMA(gfin, gfin_d.to_broadcast([128, DM])), w=["gfin"], chan="gf")
            wgv, wuv_, wdv_ = wview(wg), wview(wu), wview(wd)
            NP = FC // 2

            def load_gu(gi):
                s = gi % 2
                c0 = (gi % NP) * 256
                P.dma(DMA(wgu[:, s, 0, :, :], wgv[:, :, c0:c0 + 256]), r=[("W", wg)], w=[("wgu", s, 0)], chan="gu%d" % s)
                P.dma(DMA(wgu[:, s, 1, :, :], wuv_[:, :, c0:c0 + 256]), r=[("W", wu)], w=[("wgu", s, 1)], chan="gu%d" % s)

            load_gu(0)
            gi = 0
            for b in range(NB):
                for tt in range(4):
                    t = b * 4 + tt
                    norm_T(xs[:, t, :], DM, [("xs", t)], xn[:, tt % 2, :], [("xn", tt % 2)], hT, [("hT", t)], t * 128, "act")
                for h2 in range(2):
                    P.dma(DMA(wdt[:, h2 * 11:(h2 + 1) * 11, :], wdv_[:, h2 * 11:(h2 + 1) * 11, :]),
                          r=[("W", wd)], w=[("wdt", h2)], chan="wd%d" % h2)
                hres = [("hT", b * 4 + tt) for tt in range(4)]
                for fp in range(NP):
                    if gi + 1 < NB * NP:
                        load_gu(gi + 1)
                    s = gi % 2
                    for fi in range(2):
                        f = fp * 2 + fi
                        gb, ub = 2 + f % 2, 4 + f % 2
                        for g, dstb in ((0, gb), (1, ub)):
                            pairs = [(wgu[:, s, g, k, fi * 128:(fi + 1) * 128], hT[:, k, b * 512:(b + 1) * 512]) for k in range(KC)]
                            P.op("pe", MM(bank(dstb), pairs), r=hres + [("wgu", s, 0), ("wgu", s, 1)], w=[PB(dstb)])
                        P.op("act", ACT(sg[:, f % 2, :], bank(gb), AF.Silu), w=[PB(gb), ("sg", f % 2)])
                        P.op("dve", TT("dve", aT[:, f, :], bank(ub), sg[:, f % 2, :], ALU.mult), r=[("sg", f % 2)], w=[PB(ub), ("aT", f)])
                    gi += 1
                ares = [("aT", f) for f in range(FC)]
                for tt in range(4):
                    t = b * 4 + tt
                    for hf in range(2):
                        db = 6 + (tt * 2 + hf) % 2
                        pairs = [(aT[:, f, tt * 128:(tt + 1) * 128], wdt[:, f, hf * 512:(hf + 1) * 512]) for f in range(FC)]
                        P.op("pe", MM(bank(db), pairs), r=ares + [("wdt", 0), ("wdt", 1)], w=[PB(db)])
                        xsl = xs[:, t, hf * 512:(hf + 1) * 512]
                        P.op("dve", STT(xsl, bank(db), 0.5, xsl, ALU.mult, ALU.add), w=[PB(db), ("xs", t)])
                    if final:
                        rstd, rres = rms_rstd(xs[:, t, :], DM, xn[:, tt % 2, :], [("xs", t)], [("xn", tt % 2)])
                        P.op("dve", STT(xs[:, t, :], xs[:, t, :], rstd, gfin, ALU.mult, ALU.mult), r=[rres, "gfin"], w=[("xs", t)])
                        P.dma(DMA(y_d[si, t * 128:(t + 1) * 128, :], xs[:, t, :]), r=[("xs", t)], w=[("y", si, t)], chan="yst%d" % t)
                        out_keys.append(("y", si, t))
                        if si + 1 < nseq:
                            load_x(si + 1, t)
            P.barrier()

        def group_out(wrow0, mixedT, wo):
            P.op("dve", TS(rstd_g[:, 1, :], bank(6, 16), 1.0 / 512, EPS, ALU.mult, ALU.add), w=[PB(6), "rg1"])
            P.op("pool", TT("pool", rstd_g[:, 2, :], rstd_g[:, 1, :], nhalf[:, 0:16], ALU.pow), r=["rg1", "nhalf"], w=["rg2"])
            wv = wview("wout")
            c0 = wrow0 // 128
            for h2 in range(2):
                P.dma(DMA(wo[:, h2 * 2:(h2 + 1) * 2, :], wv[:, c0 + h2 * 2:c0 + (h2 + 1) * 2, :]),
                      r=[("W", "wout")], w=[("wo", h2)], chan="wo%d" % h2)
            mres = [("mx", j, b) for j in range(4) for b in range(NB)]
            for t in range(NT):
                for hf in range(2):
                    db = (t * 2 + hf) % 2
                    pairs = [(mixedT[:, j, t * 128:(t + 1) * 128], wo[:, j, hf * 512:(hf + 1) * 512]) for j in range(4)]
                    P.op("pe", MM(bank(db), pairs), r=mres + [("wo", 0), ("wo", 1)], w=[PB(db)])
                    xsl = xs[:, t, hf * 512:(hf + 1) * 512]
                    P.op("dve", STT(xsl, bank(db), rstd_g[:, 2, t:t + 1], xsl, ALU.mult, ALU.add), r=["rg2"], w=[PB(db), ("xs", t)])

        def sumsq(mixedT, h, b, first, sq):
            r0 = 64 * (h % 2)
            j = h // 2
            msl = mixedT[r0:r0 + 64, j, b * 512:(b + 1) * 512]
            P.op("pool", TT("pool", sq[r0:r0 + 64, b % 2, :], msl, msl, ALU.mult), r=[("mx", j, b)], w=[("sq", b % 2)])
            items = []
            for tt in range(4):
                t = b * 4 + tt
                items.append((ps[:, 6 * 512 + t:6 * 512 + t + 1], sq[r0:r0 + 64, b % 2, tt * 128:(tt + 1) * 128],
                              ones_b[r0:r0 + 64, 0:1], bool(first and tt == 0), False))
            P.op("pe", MMS(items), r=[("sq", b % 2), "ones_b"], w=[PB(6)])

        def mixer(si):
            A.reset()
            mixedT = A.bf(4, SEQ)
            wo = A.bf(4, DM)
            PT = A.bf(3, 1024)
            tmp = A.f32(2, 512)
            rden = A.f32(2, 512)
            sq = A.bf(2, 512)
            base = A.off
            tabm = A.f32(SEQ)
            cqT = A.bf(3, SEQ)
            ckvT = A.bf(2, SEQ)
            kT = A.bf(SEQ)
            qT = A.bf(SEQ)
            vaug = A.bf(NT, 128)
            wuq = A.bf(3, 1024)
            wuk = A.bf(2, 512)
            wuv = A.bf(2, 512)
            wlat = A.bf(KC, 704)
            P.dma(DMA(tabm, tabm_d[:, :]), w=["tabm"], chan="tb")
            P.dma(DMA(wlat, wview("wlat")), r=[("W", "wlat")], w=["wlat"], chan="wl")
            P.dma(DMA(wuq, wview("wuq")), r=[("W", "wuq")], w=["wuq"], chan="wq")
            P.dma(DMA(wuk, wview("wuk")), r=[("W", "wuk")], w=["wuk"], chan="wk")
            P.dma(DMA(wuv, wview("wuv")), r=[("W", "wuv")], w=["wuv"], chan="wv")
            P.op("pool", (lambda: nc.gpsimd.memset(vaug[:, :, 64:128], 1.0)), w=["vaug1"])
            for t in range(NT):
                norm_T(xs[:, t, :], DM, [("xs", t)], xn_big[:, t % 2, :], [("xnb", t % 2)], hT, [("hT", t)], t * 128, "act")
            for t in range(NT):
                qb_, kb_ = 2 + t % 2, 4 + t % 2
                pq = [(hT[:, k, t * 128:(t + 1) * 128], wlat[:, k, 0:384]) for k in range(KC)]
                pk = [(hT[:, k, t * 128:(t + 1) * 128], wlat[:, k, 384:640]) for k in range(KC)]
                P.op("pe", MM(bank(qb_, 384), pq), r=[("hT", t), "wlat"], w=[PB(qb_)])
                P.op("pe", MM(bank(kb_, 256), pk), r=[("hT", t), "wlat"], w=[PB(kb_)])
                norm_T(bank(qb_, 384), 384, [PB(qb_)], xn_big[:, 0, 0:384], [("xnb", 0)], cqT, [("cqT", t // 4)], t * 128, "dve")
                norm_T(bank(kb_, 256), 256, [PB(kb_)], xn_big[:, 1, 0:256], [("xnb", 1)], ckvT, [("ckvT", t // 4)], t * 128, "dve")
            for b in range(NB):
                kb_ = 6 + b % 2
                pr = [(wlat[:, k, 640:704], hT[:, k, b * 512:(b + 1) * 512]) for k in range(KC)]
                P.op("pe", MM(bank(kb_, 512, 64), pr), r=[("hT", 4 * b + i) for i in range(4)] + ["wlat"], w=[PB(kb_)])
                P.op("dve", TT("dve", tmp[0:32, b % 2, :], bank(kb_, 512, 32), tabm[0:32, b * 512:(b + 1) * 512], ALU.mult),
                     r=["tabm"], w=[PB(kb_), ("tmp", b % 2)])
                P.op("dve", TT("dve", rden[0:32, b % 2, :], bank(kb_, 512, 32, 32), tabm[32:64, b * 512:(b + 1) * 512], ALU.mult),
                     r=["tabm"], w=[PB(kb_), ("rden", b % 2)])
                P.op("pool", TT("pool", kT[64:96, b * 512:(b + 1) * 512], tmp[0:32, b % 2, :], rden[0:32, b % 2, :], ALU.add),
                     r=[("tmp", b % 2), ("rden", b % 2)], w=[("kTr", b)])
            P.barrier()
            scale = 96 ** -0.5
            stage = 0
            for h in range(8):
                for b in range(NB):
                    qb_, kb_ = 0 + b % 2, 2 + b % 2
                    bs = slice(b * 512, (b + 1) * 512)
                    pq = [(wuq[:, k, h * 128:(h + 1) * 128], cqT[:, k, bs]) for k in range(3)]
                    P.op("pe", MM(bank(qb_), pq), r=[("cqT", b), "wuq"], w=[PB(qb_)])
                    P.op("act", ACOPY(qT[0:64, bs], bank(qb_, 512, 64)), w=[PB(qb_), ("qTn", b)])
                    P.op("dve", TT("dve", tmp[64:96, b % 2, :], bank(qb_, 512, 32, 64), tabm[64:96, bs], ALU.mult),
                         r=["tabm"], w=[PB(qb_), ("tmp", b % 2)])
                    P.op("dve", TT("dve", rden[64:96, b % 2, :], bank(qb_, 512, 32, 96), tabm[96:128, bs], ALU.mult),
                         r=["tabm"], w=[PB(qb_), ("rden", b % 2)])
                    P.op("pool", TT("pool", qT[64:96, bs], tmp[64:96, b % 2, :], rden[64:96, b % 2, :], ALU.add),
                         r=[("tmp", b % 2), ("rden", b % 2)], w=[("qTr", b)])
                    pk = [(wuk[:, k, h * 64:(h + 1) * 64], ckvT[:, k, bs]) for k in range(2)]
                    P.op("pe", MM(bank(kb_, 512, 64), pk), r=[("ckvT", b), "wuk"], w=[PB(kb_)])
                    P.op("act", ACOPY(kT[0:64, bs], bank(kb_, 512, 64)), w=[PB(kb_), ("kTn", b)])
                for g in range(2):
                    vb_ = 4 + g
                    items = []
                    for tt in range(8):
                        t = g * 8 + tt
                        for k in range(2):
                            items.append((ps[:, vb_ * 512 + tt * 64:vb_ * 512 + (tt + 1) * 64], ckvT[:, k, t * 128:(t + 1) * 128],
                                          wuv[:, k, h * 64:(h + 1) * 64], bool(tt == 0 and k == 0), bool(k == 1)))
                    P.op("pe", MMS(items), r=[("ckvT", g * 2), ("ckvT", g * 2 + 1), "wuv"], w=[PB(vb_)])
                    P.op("dve", VCOPY(vaug[:, g * 8:(g + 1) * 8, 0:64], bank(vb_).rearrange("p (t c) -> p t c", t=8)),
                         w=[PB(vb_), ("vaug", g)])
                kres = [("kTn", b) for b in range(NB)] + [("kTr", b) for b in range(NB)]
                vres = [("vaug", 0), ("vaug", 1), "vaug1"]
                stg = [(qb, cp) for qb in range(NB) for cp in range(8)]

                def emit_S(i, stage0=stage):
                    qb, cp = stg[i]
                    sbk = ((stage0 + i) % 2) * 2
                    qs = slice(qb * 512, (qb + 1) * 512)
                    items = [(bank(sbk + j), kT[0:96, (cp * 2 + j) * 128:(cp * 2 + j + 1) * 128], qT[0:96, qs], True, True)
                             for j in range(2)]
                    P.op("pe", MMS(items), r=kres + [("qTn", qb), ("qTr", qb)], w=[PB(sbk), PB(sbk + 1)])

                emit_S(0)
                for i, (qb, cp) in enumerate(stg):
                    if i + 1 < len(stg):
                        emit_S(i + 1)
                    sbk = ((stage + i) % 2) * 2
                    st = (stage + i) % 3
                    ob = 4 + qb % 2
                    qs = slice(qb * 512, (qb + 1) * 512)
                    P.op("act", ACT(PT[:, st, :], ps[:, sbk * 512:(sbk + 2) * 512], AF.Exp, scale=scale),
                         w=[PB(sbk), PB(sbk + 1), ("PT", st)])
                    items = []
                    for j in range(2):
                        c = cp * 2 + j
                        items.append((bank(ob), vaug[:, c, :], PT[:, st, j * 512:(j + 1) * 512], bool(c == 0), bool(c == NT - 1)))
                    P.op("pe", MMS(items), r=vres + [("PT", st)], w=[PB(ob)])
                    if cp == 7:
                        r0 = 64 * (h % 2)
                        P.op("dve", RECIP(rden[64:128, qb % 2, :], bank(ob, 512, 64, 64)), w=[PB(ob), ("rden", qb % 2)])
                        P.op("dve", TT("dve", mixedT[r0:r0 + 64, h // 2, qs], bank(ob, 512, 64), rden[64:128, qb % 2, :], ALU.mult),
                             r=[("rden", qb % 2)], w=[PB(ob), ("mx", h // 2, qb)])
                        sumsq(mixedT, h, qb, first=(h == 0 and qb == 0), sq=sq)
                stage += len(stg)
            group_out(0, mixedT, wo)
            P.barrier()
            A.reset(base)
            tabd = A.f32(SEQ)
            qdT = A.bf(SEQ)
            kdT = A.bf(SEQ)
            acc = A.f32(SEQ)
            wdqk = A.bf(KC, 160)
            wdv = A.bf(KC, 256)
            vd = A.bf(3 * NT * 4, 65)
            P.dma(DMA(tabd, tabd_d[:, :]), w=["tabd"], chan="tb")
            P.op("pool", (lambda: nc.gpsimd.memset(vd[:, :, 64:65], 1.0)), w=["vd1"])
            dscale = 0.125
            stage = 0
            mcnt = 0
            allh = [("hT", i) for i in range(NT)]
            for h in range(8):
                g, hh = h // 4, h % 4
                if hh == 0:
                    P.dma(DMA(wdv, wview("wdv")[:, :, g * 256:(g + 1) * 256]), r=[("W", "wdv")], w=["wdv"], chan="dv")
                    vcnt = 0
                    for o, d in enumerate((1, 4, 16)):
                        L = SEQ // d
                        for r in range(d):
                            for c in range(L // 128):
                                tl = (r * L) // 128 + c
                                vb_ = 5 if vcnt % 2 else 7
                                vcnt += 1
                                pairs = []
                                for k in range(KC):
                                    lh = hT[:, k, :].rearrange("p (a b) -> p a b", b=d)[:, 128 * c:128 * c + 128, r]
                                    pairs.append((lh, wdv[:, k, :]))
                                P.op("pe", MM(bank(vb_, 256), pairs), r=allh + ["wdv"], w=[PB(vb_)])
                                src = bank(vb_, 256).rearrange("p (h c) -> p h c", h=4)
                                i0 = (o * NT + tl) * 4
                                if vcnt % 2:
                                    P.op("act", ACOPY(vd[:, i0:i0 + 4, 0:64], src), w=[PB(vb_), ("vd", o)])
                                else:
                                    P.op("dve", VCOPY(vd[:, i0:i0 + 4, 0:64], src), w=[PB(vb_), ("vd", o)])
                P.dma(DMA(wdqk, wview("wdqk")[:, :, h * 160:(h + 1) * 160]), r=[("W", "wdqk")], w=["wdqk"], chan="dq")
                for which, dst, nm in ((0, qdT, "qd"), (1, kdT, "kd")):
                    for b in range(NB):
                        pb_ = 4 if (b + which) % 2 == 0 else 7
                        bs = slice(b * 512, (b + 1) * 512)
                        pairs = [(wdqk[:, k, which * 80:(which + 1) * 80], hT[:, k, bs]) for k in range(KC)]
                        P.op("pe", MM(bank(pb_, 512, 80), pairs), r=[("hT", 4 * b + i) for i in range(4)] + ["wdqk"], w=[PB(pb_)])
                        P.op("act", ACOPY(dst[0:64, bs], bank(pb_, 512, 64)), w=[PB(pb_), (nm, b)])
                        P.op("dve", TT("dve", tmp[0:16, b % 2, :], bank(pb_, 512, 16), tabd[0:16, bs], ALU.mult),
                             r=["tabd"], w=[PB(pb_), ("tmp", b % 2)])
                        P.op("dve", TT("dve", rden[0:16, b % 2, :], bank(pb_, 512, 16, 64), tabd[64:80, bs], ALU.mult),
                             r=["tabd"], w=[PB(pb_), ("rden", b % 2)])
                        P.op("pool", TT("pool", dst[0:16, bs], tmp[0:16, b % 2, :], rden[0:16, b % 2, :], ALU.add),
                             r=[("tmp", b % 2), ("rden", b % 2)], w=[(nm, b)])
                qkres = [("qd", b) for b in range(NB)] + [("kd", b) for b in range(NB)]
                chunks = []
                for o, d in enumerate((1, 4, 16)):
                    L = SEQ // d
                    nch = L // 128
                    for r in range(d):
                        for c in range(nch):
                            lo, hi = max(0, 128 * c - 64), min(L, 128 * c + 192)
                            chunks.append(dict(o=o, d=d, L=L, nch=nch, r=r, c=c, lo=lo, hi=hi, n=hi - lo, mlo=lo - (128 * c - 64)))
                dstg = [chunks[i:i + 2] for i in range(0, len(chunks), 2)]
                started = set()

                def emit_dS(i, stage0=stage):
                    sbk = (stage0 + i) % 3
                    items = []
                    for j, ch in enumerate(dstg[i]):
                        d, r, c = ch["d"], ch["r"], ch["c"]
                        kv = kdT[0:64, :].rearrange("p (a b) -> p a b", b=d)
                        qv = qdT[0:64, :].rearrange("p (a b) -> p a b", b=d)
                        c0 = 256 * j + ch["mlo"]
                        items.append((ps[:, sbk * 512 + c0:sbk * 512 + c0 + ch["n"]], kv[:, 128 * c:128 * c + 128, r],
                                      qv[:, ch["lo"]:ch["hi"], r], True, True))
                    P.op("pe", MMS(items), r=qkres, w=[PB(sbk)])

                emit_dS(0)
                emit_dS(1)
                for i, pair in enumerate(dstg):
                    if i + 2 < len(dstg):
                        emit_dS(i + 2)
                    sbk = (stage + i) % 3
                    st = (stage + i) % 3
                    P.op("act", ACT(PT[:, st, 512:1024], bank(sbk), AF.Exp, scale=dscale), w=[PB(sbk), ("PTe", st)])
                    me = "pool" if mcnt % 2 else "dve"
                    mcnt += 1
                    P.op(me, TT(me, PT[:, st, 0:512], PT[:, st, 512:1024], mask[:, 0:512], ALU.mult),
                         r=[("PTe", st), "mask"], w=[("PT", st)])
                    for j, ch in enumerate(pair):
                        o, d, L, nch, r, c, lo, n = ch["o"], ch["d"], ch["L"], ch["nch"], ch["r"], ch["c"], ch["lo"], ch["n"]
                        c0 = 256 * j + ch["mlo"]
                        tl = (r * L) // 128 + c
                        p_lo = r * L + lo
                        a = p_lo
                        while a < p_lo + n:
                            e = min(p_lo + n, (a // 512 + 1) * 512)
                            bk = a // 512
                            ob = 3 + bk % 2
                            first = (o, bk) not in started
                            started.add((o, bk))
                            i0 = (o * NT + tl) * 4 + hh
                            P.op("pe", MMS([(ps[0:65, ob * 512 + a % 512:ob * 512 + a % 512 + (e - a)], vd[:, i0, :],
                                             PT[:, st, c0 + a - p_lo:c0 + e - p_lo], bool(first), False)]),
                                 r=[("PT", st), ("vd", o), "vd1"], w=[PB(ob)])
                            a = e
                        pend = r * L + min(L, 128 * c + 192)
                        done = []
                        if c == nch - 1:
                            if pend % 512 == 0:
                                done.append(pend // 512 - 1)
                        elif d == 1 and c % 4 == 0 and c > 0:
                            done.append(c // 4 - 1)
                        for bk in done:
                            ob = 3 + bk % 2
                            if d == 1:
                                P.op("dve", VCOPY(acc[0:65, bk * 512:(bk + 1) * 512], bank(ob, 512, 65)), w=[PB(ob), "acc"])
                            elif d == 4:
                                dstv = acc[0:65, :].rearrange("p (a b) -> p a b", b=4)[:, :, bk]
                                P.op("dve", TT("dve", dstv, bank(ob, 512, 65), dstv, ALU.add), w=[PB(ob), "acc"])
                            else:
                                dstv = acc[0:65, :].rearrange("p (a b) -> p b a", b=16)[:, bk * 4:(bk + 1) * 4, :]
                                srcv = bank(ob, 512, 65).rearrange("p (r a) -> p r a", r=4)
                                P.op("dve", TT("dve", dstv, srcv, dstv, ALU.add), w=[PB(ob), "acc"])
                stage += len(dstg)
                accd = acc[64:65, :].rearrange("p (a b) -> p a b", a=4)
                P.op("dve", RECIP(rden[64:65, :, :], accd[:, 0:2, :]), r=["acc"], w=[("rden", 0), ("rden", 1)])
                P.op("dve", RECIP(tmp[64:65, :, :], accd[:, 2:4, :]), r=["acc"], w=[("tmp", 0), ("tmp", 1)])
                r0 = 64 * (h % 2)
                for b in range(NB):
                    rsrc = (rden if b < 2 else tmp)[64:65, b % 2, :]
                    rres = ("rden" if b < 2 else "tmp", b % 2)
                    bb = 5 if b % 2 == 0 else 7
                    bs = slice(b * 512, (b + 1) * 512)
                    P.op("pe", MM(bank(bb, 512, 64), [(ones_f[64:65, 0:64], rsrc)]), r=[rres, "ones_f"], w=[PB(bb)])
                    P.op("dve", TT("dve", mixedT[r0:r0 + 64, h // 2, bs], acc[0:64, bs], bank(bb, 512, 64), ALU.mult),
                         r=["acc"], w=[PB(bb), ("mx", h // 2, b)])
                    sumsq(mixedT, h, b, first=(h == 0 and b == 0), sq=sq)
            group_out(512, mixedT, wo)
            P.barrier()

        xn_big = sb("xn_big", [128, 2, DM], BF16)

        for si in range(nseq):
            ffn(si, "wg1", "wu1", "wd1", final=False)
            if dbg_stage != "ffn":
                mixer(si)
            ffn(si, "wg2", "wu2", "wd2", final=True)
        P.op("sp", lambda: nc.sync.nop(), r=list(out_keys))
        P.emit(stack)
        print("PROG ops", P.stats[0], "sig", P.stats[1], "arena peak", A.peak, flush=True)
    return nc


def _rope_tables():
    pos = np.arange(SEQ, dtype=np.float32)

    def cs(rot):
        half = rot // 2
        inv = np.float32(ROPE_THETA) ** (-np.arange(half, dtype=np.float32) * np.float32(2.0 / rot))
        ang = pos[None, :] * inv.astype(np.float32)[:, None]
        return np.cos(ang).astype(np.float32), np.sin(ang).astype(np.float32)

    cm, sm = cs(32)
    cd, sd = cs(16)
    tabm = np.ones((128, SEQ), np.float32)
    for base in (0, 64):
        tabm[base:base + 16] = cm
        tabm[base + 16:base + 32] = cm
        tabm[base + 32:base + 48] = -sm
        tabm[base + 48:base + 64] = sm
    tabd = np.ones((128, SEQ), np.float32)
    tabd[0:8] = cd
    tabd[8:16] = cd
    tabd[64:72] = -sd
    tabd[72:80] = sd
    return tabm, tabd


def _host_layout(inp):
    g = lambda k: np.asarray(inp[k], np.float32)
    w_in = g("w_in")[0]
    sw32 = np.r_[16:32, 0:16]
    kr = w_in[:, 640:672]
    wlat = np.concatenate([w_in[:, 0:640], kr, kr[:, sw32]], axis=1)
    wuq_src = g("mla_w_uq")[0].reshape(384, 8, 96)
    wuq = np.concatenate([wuq_src, wuq_src[:, :, 64:96][:, :, sw32]], axis=2).reshape(384, 1024)
    wukv = g("mla_w_ukv")[0].reshape(256, 8, 128)
    wuk = wukv[:, :, 0:64].reshape(256, 512)
    wuv = wukv[:, :, 64:128].reshape(256, 512)
    qkv = w_in[:, 672:].reshape(DM, 3, 8, 64)
    sw16 = np.r_[8:16, 0:8]
    qd, kd = qkv[:, 0], qkv[:, 1]
    wdqk = np.concatenate([qd, qd[:, :, sw16], kd, kd[:, :, sw16]], axis=2).reshape(DM, 1280)
    wdv = qkv[:, 2].reshape(DM, 512)

    def fm(v):
        v = np.asarray(v, np.float32).reshape(-1)
        return v.reshape(-1, 128).T

    gains = np.concatenate([fm(g("ffn1_norm")), fm(g("mix_norm")), fm(g("mla_q_norm")), fm(g("mla_kv_norm")),
                            fm(g("mla_out_norm")), fm(g("dil_out_norm")), fm(g("ffn2_norm"))], axis=1)
    assert gains.shape == (128, NG)
    tabm, tabd = _rope_tables()
    jj = np.arange(128)[:, None]
    ii = np.arange(256)[None, :]
    mask = ((ii - jj >= 0) & (ii - jj <= 128)).astype(np.float32)
    mask = np.concatenate([mask, mask], axis=1).astype(ml_dtypes.bfloat16)
    ident = np.eye(128, dtype=np.float32).astype(ml_dtypes.bfloat16)
    c = np.ascontiguousarray
    shared = {
        "wg1": c(g("ffn1_w_gate")[0]), "wu1": c(g("ffn1_w_up")[0]), "wd1": c(g("ffn1_w_down")[0]),
        "wg2": c(g("ffn2_w_gate")[0]), "wu2": c(g("ffn2_w_up")[0]), "wd2": c(g("ffn2_w_down")[0]),
        "wlat": c(wlat), "wuq": c(wuq), "wuk": c(wuk), "wuv": c(wuv), "wdqk": c(wdqk), "wdv": c(wdv),
        "wout": c(g("w_out")[0]), "gains": c(gains), "gfin": c(g("final_norm").reshape(1, DM)),
        "tabm": tabm, "tabd": tabd, "mask": c(mask), "ident": c(ident),
    }
    return shared


def kernel(**inputs):
    x = np.asarray(inputs["x"], np.float32)
    B = x.shape[0]
    nseq = B // N_CORES
    shared = _host_layout(inputs)
    nc = build_program(nseq)
    in_maps = []
    for c in range(N_CORES):
        m = dict(shared)
        m["x"] = np.ascontiguousarray(x[c * nseq:(c + 1) * nseq])
        in_maps.append(m)
    res = run_bass_kernel_spmd(nc, in_maps, core_ids=list(range(N_CORES)))
    out = np.concatenate([np.asarray(r["y"], np.float32) for r in res.results], axis=0)
    return out
```

```python
import math
from contextlib import ExitStack

import numpy as np
import ml_dtypes

import concourse.bass as bass
import concourse.mybir as mybir
from concourse.bass_utils import run_bass_kernel_spmd

F32 = mybir.dt.float32
BF16 = mybir.dt.bfloat16
AF = mybir.ActivationFunctionType
ALU = mybir.AluOpType

N_CORES = 8
SEQ = 2048
DM = 1024
DFF = 2816
NT = SEQ // 128
NB = SEQ // 512
KC = DM // 128
FC = DFF // 128
EPS = 1e-6
ROPE_THETA = 500000.0
USE_FAST_RECIP = False

G1, GMIX, GQ, GKV, GMO, GDO, G2 = 0, 8, 16, 19, 21, 25, 29
NG = 37


class Prog:
    ENGS = ("pe", "act", "dve", "pool", "sp")

    def __init__(self, nc):
        self.nc = nc
        self.E = {"pe": nc.tensor, "act": nc.scalar, "dve": nc.vector, "pool": nc.gpsimd, "sp": nc.sync}
        self.ops = []
        self.last_w = {}
        self.readers = {}
        self.bar = set()
        self.last_eng = {}
        self.last_chan = {}

    def _add(self, eng, fn, r, w, chan=None):
        i = len(self.ops)
        deps = set(self.bar)
        for k in r:
            if k in self.last_w:
                deps.add(self.last_w[k])
        for k in w:
            if k in self.last_w:
                deps.add(self.last_w[k])
            deps.update(self.readers.get(k, ()))
        for k in r:
            self.readers.setdefault(k, []).append(i)
        for k in w:
            self.last_w[k] = i
            self.readers[k] = []
        self.ops.append([eng, fn, deps, chan, None])
        if chan is None:
            self.last_eng[eng] = i
        else:
            self.last_chan[chan] = i
        return i

    def op(self, eng, fn, r=(), w=()):
        return self._add(eng, fn, r, w)

    def dma(self, fn, r=(), w=(), chan=None, q="sp"):
        assert chan is not None
        return self._add(q, fn, r, w, chan)

    def barrier(self):
        self.bar = set(self.last_eng.values()) | set(self.last_chan.values())

    def emit(self, stack):
        nc = self.nc
        ops = self.ops
        has_dep = [False] * len(ops)
        for o in ops:
            for d in o[2]:
                has_dep[d] = True
        cnt = {e: 0 for e in self.ENGS}
        ccnt = {}
        for i, o in enumerate(ops):
            if o[3] is not None:
                ccnt[o[3]] = ccnt.get(o[3], 0) + 16
                o[4] = (("c", o[3]), ccnt[o[3]])
            elif has_dep[i]:
                cnt[o[0]] += 1
                o[4] = (("e", o[0]), cnt[o[0]])
        sems = {}
        for e in self.ENGS:
            if cnt[e]:
                sems[("e", e)] = stack.enter_context(nc.semaphore("s_" + e))
        for c in ccnt:
            sems[("c", c)] = stack.enter_context(nc.semaphore("c_" + c))
        waited = {e: {} for e in self.ENGS}
        for i, o in enumerate(ops):
            eng, fn, deps, chan, sig = o
            need = {}
            for d in deps:
                od = ops[d]
                if od[3] is None and od[0] == eng and eng in ("pe", "sp"):
                    continue
                s, v = od[4]
                if need.get(s, 0) < v:
                    need[s] = v
            E = self.E[eng]
            for s, v in need.items():
                if waited[eng].get(s, 0) >= v:
                    continue
                waited[eng][s] = v
                E.wait_ge(sems[s], v)
            ins = fn()
            if sig is not None:
                ins.then_inc(sems[sig[0]], 16 if chan is not None else 1)
        self.stats = (len(ops), dict(cnt), dict(ccnt))


class Arena:
    def __init__(self, t, n):
        self.t, self.n, self.off = t, n, 0
        self.peak = 0

    def reset(self, off=0):
        self.off = off

    def _take(self, n):
        a = self.off
        self.off += n
        self.peak = max(self.peak, self.off)
        assert self.off <= self.n, ("arena overflow", self.off, self.n)
        return a

    def bf(self, *shape):
        n = int(np.prod(shape))
        a = self._take(n)
        ap = self.t[:, a:a + n]
        if len(shape) == 2:
            return ap.rearrange("p (a b) -> p a b", a=shape[0])
        if len(shape) == 3:
            return ap.rearrange("p (a b c) -> p a b c", a=shape[0], b=shape[1])
        return ap

    def f32(self, *shape):
        self.off = (self.off + 1) // 2 * 2
        n = int(np.prod(shape))
        a = self._take(2 * n)
        ap = self.t[:, a:a + 2 * n].bitcast(F32)
        if len(shape) == 2:
            return ap.rearrange("p (a b) -> p a b", a=shape[0])
        return ap


def build_program(nseq, dbg_stage=None):
    nc = bass.Bass("TRN2", target_bir_lowering=False)
    P = Prog(nc)

    def din(name, shape, dt=F32):
        return nc.dram_tensor(name, list(shape), dt, kind="ExternalInput").ap()

    def dscr(name, shape):
        return nc.dram_tensor(name, list(shape), BF16, kind="Internal").ap()

    x_d = din("x", [nseq, SEQ, DM])
    y_d = nc.dram_tensor("y", [nseq, SEQ, DM], F32, kind="ExternalOutput").ap()
    wsrc = {
        "wg1": din("wg1", [DM, DFF]), "wu1": din("wu1", [DM, DFF]), "wd1": din("wd1", [DFF, DM]),
        "wg2": din("wg2", [DM, DFF]), "wu2": din("wu2", [DM, DFF]), "wd2": din("wd2", [DFF, DM]),
        "wlat": din("wlat", [DM, 704]), "wuq": din("wuq", [384, 1024]),
        "wuk": din("wuk", [256, 512]), "wuv": din("wuv", [256, 512]),
        "wdqk": din("wdqk", [DM, 1280]), "wdv": din("wdv", [DM, 512]), "wout": din("wout", [DM, DM]),
    }
    wgain = {"wg1": G1, "wu1": G1, "wd1": None, "wg2": G2, "wu2": G2, "wd2": None, "wlat": GMIX, "wuq": GQ,
             "wuk": GKV, "wuv": GKV, "wdqk": GMIX, "wdv": GMIX, "wout": GMO}
    wb = {k: dscr(k + "_b", v.shape) for k, v in wsrc.items()}
    gains_d = din("gains", [128, NG])
    gfin_d = din("gfin", [1, DM])
    tabm_d = din("tabm", [128, SEQ])
    tabd_d = din("tabd", [128, SEQ])
    mask_d = din("mask", [128, 512], BF16)
    ident_d = din("ident", [128, 128], BF16)

    stack = ExitStack()
    with stack:
        sb = lambda name, shape, dt: stack.enter_context(nc.sbuf_tensor("sb_" + name, shape, dt))
        xs = sb("xs", [128, NT, DM], F32)
        hT = sb("hT", [128, KC, SEQ], BF16)
        ident = sb("ident", [128, 128], BF16)
        ones_b = sb("ones_b", [128, 8], BF16)
        ones_f = sb("ones_f", [128, 64], F32)
        nhalf = sb("nhalf", [128, 16], F32)
        gains = sb("gains", [128, NG], F32)
        mask = sb("mask", [128, 512], BF16)
        stat = sb("stat", [128, 3, 16], F32)
        rstd_g = sb("rstd_g", [128, 3, 16], F32)
        ARENA_N = 51800
        arena_t = sb("arena", [128, ARENA_N], BF16)
        ps = stack.enter_context(nc.psum_tensor("ps", [128, 8 * 512], F32))
        A = Arena(arena_t, ARENA_N)

        def bank(b, n=512, rows=128, r0=0):
            return ps[r0:r0 + rows, b * 512:b * 512 + n]

        def bank_bf(b, n=1024):
            return ps[:, b * 512:(b + 1) * 512].bitcast(BF16)[:, 0:n]

        PB = lambda b: ("ps", b)

        P.dma(lambda: nc.sync.dma_start(out=ident[:], in_=ident_d[:, :]), w=["ident"], chan="k0")
        P.dma(lambda: nc.sync.dma_start(out=gains[:], in_=gains_d[:, :]), w=["gains"], chan="k1")
        P.dma(lambda: nc.sync.dma_start(out=mask[:], in_=mask_d[:, :]), w=["mask"], chan="k2")
        P.op("pool", lambda: nc.gpsimd.memset(ones_b[:], 1.0), w=["ones_b"])
        P.op("pool", lambda: nc.gpsimd.memset(ones_f[:], 1.0), w=["ones_f"])
        P.op("pool", lambda: nc.gpsimd.memset(nhalf[:], -0.5), w=["nhalf"])

        for t in range(NT):
            P.dma((lambda t=t: nc.sync.dma_start(out=xs[:, t, :], in_=x_d[0, t * 128:(t + 1) * 128, :])), w=[("xs", t)], chan="xl%d" % t)
        A.reset()
        stg_f = A.f32(4, DFF)
        stg_b = A.bf(4, DFF)
        it = 0
        for name, src in wsrc.items():
            K, N = src.shape
            for j in range(K // 128):
                s = it % 4
                gcol = wgain[name]
                if name == "wout" and j >= 4:
                    gcol = GDO - 4
                P.dma(lambda s=s, src=src, j=j, N=N: nc.sync.dma_start(out=stg_f[:, s, 0:N], in_=src[j * 128:(j + 1) * 128, :]),
                      w=[("stf", s)], chan="il%d" % s)
                if gcol is None:
                    if it % 2 == 0:
                        P.op("act", lambda s=s, N=N: nc.scalar.copy(out=stg_b[:, s, 0:N], in_=stg_f[:, s, 0:N]),
                             r=[("stf", s)], w=[("stb", s)])
                    else:
                        P.op("dve", lambda s=s, N=N: nc.vector.tensor_copy(out=stg_b[:, s, 0:N], in_=stg_f[:, s, 0:N]),
                             r=[("stf", s)], w=[("stb", s)])
                else:
                    gc = gcol + j
                    if it % 2 == 0:
                        P.op("act", lambda s=s, N=N, gc=gc: nc.scalar.activation(out=stg_b[:, s, 0:N], in_=stg_f[:, s, 0:N],
                                                                                 func=AF.Copy, scale=gains[:, gc:gc + 1]),
                             r=[("stf", s), "gains"], w=[("stb", s)])
                    else:
                        P.op("dve", lambda s=s, N=N, gc=gc: nc.vector.tensor_scalar(out=stg_b[:, s, 0:N], in0=stg_f[:, s, 0:N],
                                                                                    scalar1=gains[:, gc:gc + 1], scalar2=None,
                                                                                    op0=ALU.mult),
                             r=[("stf", s), "gains"], w=[("stb", s)])
                P.dma(lambda s=s, name=name, j=j, N=N: nc.sync.dma_start(out=wb[name][j * 128:(j + 1) * 128, :], in_=stg_b[:, s, 0:N]),
                      r=[("stb", s)], w=[("W", name)], chan="is%d" % s)
                it += 1
        P.barrier()

        def MM(out, pairs, start=True, skip=False):
            def fn():
                ins = None
                n = len(pairs)
                for i, (l, r) in enumerate(pairs):
                    ins = nc.tensor.matmul(out, lhsT=l, rhs=r, start=(start and i == 0), stop=(i == n - 1),
                                           skip_group_check=skip)
                return ins
            return fn

        def MMS(items):
            def fn():
                ins = None
                for (o, l, r, st, sp) in items:
                    ins = nc.tensor.matmul(o, lhsT=l, rhs=r, start=st, stop=sp, skip_group_check=True)
                return ins
            return fn

        def ACT(out, in_, func, **kw):
            return lambda: nc.scalar.activation(out=out, in_=in_, func=func, **kw)

        def ACOPY(out, in_):
            return lambda: nc.scalar.copy(out=out, in_=in_)

        def TT(eng, out, in0, in1, op):
            e = nc.vector if eng == "dve" else nc.gpsimd
            return lambda: e.tensor_tensor(out=out, in0=in0, in1=in1, op=op)

        def TS(out, in0, s1, s2, op0, op1=None):
            if op1 is None:
                return lambda: nc.vector.tensor_scalar(out=out, in0=in0, scalar1=s1, scalar2=None, op0=op0)
            return lambda: nc.vector.tensor_scalar(out=out, in0=in0, scalar1=s1, scalar2=s2, op0=op0, op1=op1)

        def STT(out, in0, scalar, in1, op0, op1):
            return lambda: nc.vector.scalar_tensor_tensor(out=out, in0=in0, scalar=scalar, in1=in1, op0=op0, op1=op1)

        def VCOPY(out, in_):
            return lambda: nc.vector.tensor_copy(out=out, in_=in_)

        def RECIP(out, in_):
            return lambda: nc.vector.reciprocal(out=out, in_=in_)

        def RECIPF(out, in_):
            if USE_FAST_RECIP:
                return lambda: nc.vector.reciprocal_approx_fast(out=out, in_=in_)
            return lambda: nc.vector.reciprocal(out=out, in_=in_)

        def DMA(out, in_):
            return lambda: nc.sync.dma_start(out=out, in_=in_)

        statc = [0]

        def rms_rstd(src_ap, n, junk_ap, src_res, junk_res):
            c = statc[0] % 16
            statc[0] += 1
            P.op("act", ACT(junk_ap, src_ap, AF.Square, accum_out=stat[:, 0, c:c + 1]), r=src_res, w=junk_res + [("st0", c)])
            P.op("dve", TS(stat[:, 1, c:c + 1], stat[:, 0, c:c + 1], 1.0 / n, EPS, ALU.mult, ALU.add), r=[("st0", c)], w=[("st1", c)])
            P.op("pool", TT("pool", stat[:, 2, c:c + 1], stat[:, 1, c:c + 1], nhalf[:, 0:1], ALU.pow),
                 r=[("st1", c), "nhalf"], w=[("st2", c)])
            return stat[:, 2, c:c + 1], ("st2", c)

        tb = [0]

        def norm_T(src_ap, n, src_res, xn_ap, xn_res, dstT, dst_res, tok0, evac_eng):
            nch = n // 128
            rstd, rres = rms_rstd(src_ap, n, xn_ap, src_res, xn_res)
            P.op("dve", TS(xn_ap, src_ap, rstd, None, ALU.mult), r=src_res + [rres], w=xn_res)
            b = tb[0] % 2
            tb[0] += 1
            pst = bank_bf(b)

            def tr():
                ins = None
                for j in range(nch):
                    ins = nc.tensor.transpose(out=pst[:, j * 128:(j + 1) * 128], in_=xn_ap[:, j * 128:(j + 1) * 128],
                                              identity=ident[:])
                return ins
            P.op("pe", tr, r=xn_res + ["ident"], w=[PB(b)])
            src3 = pst[:, 0:n].rearrange("p (j c) -> p j c", j=nch)
            dst3 = dstT[:, 0:nch, tok0:tok0 + 128]
            if evac_eng == "act":
                P.op("act", ACOPY(dst3, src3), w=[PB(b)] + dst_res)
            else:
                P.op("dve", VCOPY(dst3, src3), w=[PB(b)] + dst_res)

        def wview(name):
            return wb[name].rearrange("(c p) n -> p c n", p=128)

        out_keys = []

        def load_x(si, t):
            P.dma(DMA(xs[:, t, :], x_d[si, t * 128:(t + 1) * 128, :]), w=[("xs", t)], chan="xl%d" % t)

        def ffn(si, wg, wu, wd, final):
            A.reset()
            aT = A.bf(FC, 512)
            wgu = A.bf(2, 2, KC * 256).rearrange("p s g (k c) -> p s g k c", k=KC)
            wdt = A.bf(FC, DM)
            xn = A.bf(2, DM)
            sg = A.f32(2, 512)
            gfin = A.f32(DM) if final else None
            if final:
                P.dma(DMA(gfin, gfin_d.to_broadcast([128, DM])), w=["gfin"], chan="gf")
            wgv, wuv_, wdv_ = wview(wg), wview(wu), wview(wd)
            NP = FC // 2

            def load_gu(gi):
                s = gi % 2
                c0 = (gi % NP) * 256
                P.dma(DMA(wgu[:, s, 0, :, :], wgv[:, :, c0:c0 + 256]), r=[("W", wg)], w=[("wgu", s, 0)], chan="gu%d" % s)
                P.dma(DMA(wgu[:, s, 1, :, :], wuv_[:, :, c0:c0 + 256]), r=[("W", wu)], w=[("wgu", s, 1)], chan="gu%d" % s)

            load_gu(0)
            gi = 0
            for b in range(NB):
                for tt in range(4):
                    t = b * 4 + tt
                    norm_T(xs[:, t, :], DM, [("xs", t)], xn[:, tt % 2, :], [("xn", tt % 2)], hT, [("hT", t)], t * 128, "act")
                for h2 in range(2):
                    P.dma(DMA(wdt[:, h2 * 11:(h2 + 1) * 11, :], wdv_[:, h2 * 11:(h2 + 1) * 11, :]),
                          r=[("W", wd)], w=[("wdt", h2)], chan="wd%d" % h2)
                hres = [("hT", b * 4 + tt) for tt in range(4)]
                for fp in range(NP):
                    if gi + 1 < NB * NP:
                        load_gu(gi + 1)
                    s = gi % 2
                    for fi in range(2):
                        f = fp * 2 + fi
                        gb, ub = 2 + f % 2, 4 + f % 2
                        for g, dstb in ((0, gb), (1, ub)):
                            pairs = [(wgu[:, s, g, k, fi * 128:(fi + 1) * 128], hT[:, k, b * 512:(b + 1) * 512]) for k in range(KC)]
                            P.op("pe", MM(bank(dstb), pairs), r=hres + [("wgu", s, 0), ("wgu", s, 1)], w=[PB(dstb)])
                        P.op("act", ACT(sg[:, f % 2, :], bank(gb), AF.Silu), w=[PB(gb), ("sg", f % 2)])
                        P.op("dve", TT("dve", aT[:, f, :], bank(ub), sg[:, f % 2, :], ALU.mult), r=[("sg", f % 2)], w=[PB(ub), ("aT", f)])
                    gi += 1
                ares = [("aT", f) for f in range(FC)]
                for tt in range(4):
                    t = b * 4 + tt
                    for hf in range(2):
                        db = 6 + (tt * 2 + hf) % 2
                        pairs = [(aT[:, f, tt * 128:(tt + 1) * 128], wdt[:, f, hf * 512:(hf + 1) * 512]) for f in range(FC)]
                        P.op("pe", MM(bank(db), pairs), r=ares + [("wdt", 0), ("wdt", 1)], w=[PB(db)])
                        xsl = xs[:, t, hf * 512:(hf + 1) * 512]
                        P.op("dve", STT(xsl, bank(db), 0.5, xsl, ALU.mult, ALU.add), w=[PB(db), ("xs", t)])
                    if final:
                        rstd, rres = rms_rstd(xs[:, t, :], DM, xn[:, tt % 2, :], [("xs", t)], [("xn", tt % 2)])
                        P.op("dve", STT(xs[:, t, :], xs[:, t, :], rstd, gfin, ALU.mult, ALU.mult), r=[rres, "gfin"], w=[("xs", t)])
                        P.dma(DMA(y_d[si, t * 128:(t + 1) * 128, :], xs[:, t, :]), r=[("xs", t)], w=[("y", si, t)], chan="yst%d" % t)
                        out_keys.append(("y", si, t))
                        if si + 1 < nseq:
                            load_x(si + 1, t)
            P.barrier()

        def group_out(wrow0, mixedT, wo):
            P.op("dve", TS(rstd_g[:, 1, :], bank(6, 16), 1.0 / 512, EPS, ALU.mult, ALU.add), w=[PB(6), "rg1"])
            P.op("pool", TT("pool", rstd_g[:, 2, :], rstd_g[:, 1, :], nhalf[:, 0:16], ALU.pow), r=["rg1", "nhalf"], w=["rg2"])
            wv = wview("wout")
            c0 = wrow0 // 128
            for h2 in range(2):
                P.dma(DMA(wo[:, h2 * 2:(h2 + 1) * 2, :], wv[:, c0 + h2 * 2:c0 + (h2 + 1) * 2, :]),
                      r=[("W", "wout")], w=[("wo", h2)], chan="wo%d" % h2)
            mres = [("mx", j, b) for j in range(4) for b in range(NB)]
            for t in range(NT):
                for hf in range(2):
                    db = (t * 2 + hf) % 2
                    pairs = [(mixedT[:, j, t * 128:(t + 1) * 128], wo[:, j, hf * 512:(hf + 1) * 512]) for j in range(4)]
                    P.op("pe", MM(bank(db), pairs), r=mres + [("wo", 0), ("wo", 1)], w=[PB(db)])
                    xsl = xs[:, t, hf * 512:(hf + 1) * 512]
                    P.op("dve", STT(xsl, bank(db), rstd_g[:, 2, t:t + 1], xsl, ALU.mult, ALU.add), r=["rg2"], w=[PB(db), ("xs", t)])

        def sumsq(mixedT, h, b, first, sq):
            r0 = 64 * (h % 2)
            j = h // 2
            msl = mixedT[r0:r0 + 64, j, b * 512:(b + 1) * 512]
            P.op("pool", TT("pool", sq[r0:r0 + 64, b % 2, :], msl, msl, ALU.mult), r=[("mx", j, b)], w=[("sq", b % 2)])
            items = []
            for tt in range(4):
                t = b * 4 + tt
                items.append((ps[:, 6 * 512 + t:6 * 512 + t + 1], sq[r0:r0 + 64, b % 2, tt * 128:(tt + 1) * 128],
                              ones_b[r0:r0 + 64, 0:1], bool(first and tt == 0), False))
            P.op("pe", MMS(items), r=[("sq", b % 2), "ones_b"], w=[PB(6)])

        def mixer(si):
            A.reset()
            mixedT = A.bf(4, SEQ)
            wo = A.bf(4, DM)
            PT = A.bf(3, 1024)
            tmp = A.f32(2, 512)
            rden = A.f32(2, 512)
            sq = A.bf(2, 512)
            base = A.off
            tabm = A.f32(SEQ)
            cqT = A.bf(3, SEQ)
            ckvT = A.bf(2, SEQ)
            kT = A.bf(SEQ)
            qT = A.bf(SEQ)
            vaug = A.bf(NT, 128)
            wuq = A.bf(3, 1024)
            wuk = A.bf(2, 512)
            wuv = A.bf(2, 512)
            wlat = A.bf(KC, 704)
            P.dma(DMA(tabm, tabm_d[:, :]), w=["tabm"], chan="tb")
            P.dma(DMA(wlat, wview("wlat")), r=[("W", "wlat")], w=["wlat"], chan="wl")
            P.dma(DMA(wuq, wview("wuq")), r=[("W", "wuq")], w=["wuq"], chan="wq")
            P.dma(DMA(wuk, wview("wuk")), r=[("W", "wuk")], w=["wuk"], chan="wk")
            P.dma(DMA(wuv, wview("wuv")), r=[("W", "wuv")], w=["wuv"], chan="wv")
            P.op("pool", (lambda: nc.gpsimd.memset(vaug[:, :, 64:128], 1.0)), w=["vaug1"])
            for t in range(NT):
                norm_T(xs[:, t, :], DM, [("xs", t)], xn_big[:, t % 2, :], [("xnb", t % 2)], hT, [("hT", t)], t * 128, "act")
            for t in range(NT):
                qb_, kb_ = 2 + t % 2, 4 + t % 2
                pq = [(hT[:, k, t * 128:(t + 1) * 128], wlat[:, k, 0:384]) for k in range(KC)]
                pk = [(hT[:, k, t * 128:(t + 1) * 128], wlat[:, k, 384:640]) for k in range(KC)]
                P.op("pe", MM(bank(qb_, 384), pq), r=[("hT", t), "wlat"], w=[PB(qb_)])
                P.op("pe", MM(bank(kb_, 256), pk), r=[("hT", t), "wlat"], w=[PB(kb_)])
                norm_T(bank(qb_, 384), 384, [PB(qb_)], xn_big[:, 0, 0:384], [("xnb", 0)], cqT, [("cqT", t // 4)], t * 128, "dve")
                norm_T(bank(kb_, 256), 256, [PB(kb_)], xn_big[:, 1, 0:256], [("xnb", 1)], ckvT, [("ckvT", t // 4)], t * 128, "dve")
            for b in range(NB):
                kb_ = 6 + b % 2
                pr = [(wlat[:, k, 640:704], hT[:, k, b * 512:(b + 1) * 512]) for k in range(KC)]
                P.op("pe", MM(bank(kb_, 512, 64), pr), r=[("hT", 4 * b + i) for i in range(4)] + ["wlat"], w=[PB(kb_)])
                P.op("dve", TT("dve", tmp[0:32, b % 2, :], bank(kb_, 512, 32), tabm[0:32, b * 512:(b + 1) * 512], ALU.mult),
                     r=["tabm"], w=[PB(kb_), ("tmp", b % 2)])
                P.op("dve", TT("dve", rden[0:32, b % 2, :], bank(kb_, 512, 32, 32), tabm[32:64, b * 512:(b + 1) * 512], ALU.mult),
                     r=["tabm"], w=[PB(kb_), ("rden", b % 2)])
                P.op("pool", TT("pool", kT[64:96, b * 512:(b + 1) * 512], tmp[0:32, b % 2, :], rden[0:32, b % 2, :], ALU.add),
                     r=[("tmp", b % 2), ("rden", b % 2)], w=[("kTr", b)])
            P.barrier()
            A.reset(A.off - KC * 704)
            kT2 = A.bf(SEQ)
            vaug2 = A.bf(NT, 128)
            qT2 = xn_big.rearrange("p a b -> p (a b)")
            qTs, kTs, vaugs = [qT, qT2], [kT, kT2], [vaug, vaug2]
            P.op("pool", (lambda: nc.gpsimd.memset(vaug2[:, :, 64:128], 1.0)), w=["vaug1b"])
            P.op("pool", (lambda: nc.gpsimd.tensor_copy(out=kT2[64:96, :], in_=kT[64:96, :])),
                 r=[("kTr", b) for b in range(NB)], w=[("kTr2", b) for b in range(NB)])
            scale = 96 ** -0.5
            stage = 0

            def proj_groups(h):
                p = h % 2
                qTp, kTp, vap = qTs[p], kTs[p], vaugs[p]
                groups = []
                for b in range(NB):
                    bs = slice(b * 512, (b + 1) * 512)

                    def gq(b=b, bs=bs):
                        pq = [(wuq[:, k, h * 128:(h + 1) * 128], cqT[:, k, bs]) for k in range(3)]
                        P.op("pe", MM(bank(7), pq), r=[("cqT", b), "wuq"], w=[PB(7)])
                        P.op("act", ACOPY(qTp[0:64, bs], bank(7, 512, 64)), w=[PB(7), ("qTn", p, b)])
                        P.op("dve", TT("dve", tmp[64:96, 0, :], bank(7, 512, 32, 64), tabm[64:96, bs], ALU.mult),
                             r=["tabm"], w=[PB(7), ("tmp", 0)])
                        P.op("dve", TT("dve", tmp[64:96, 1, :], bank(7, 512, 32, 96), tabm[96:128, bs], ALU.mult),
                             r=["tabm"], w=[PB(7), ("tmp", 1)])
                        P.op("pool", TT("pool", qTp[64:96, bs], tmp[64:96, 0, :], tmp[64:96, 1, :], ALU.add),
                             r=[("tmp", 0), ("tmp", 1)], w=[("qTr", p, b)])

                    def gk(b=b, bs=bs):
                        pk = [(wuk[:, k, h * 64:(h + 1) * 64], ckvT[:, k, bs]) for k in range(2)]
                        P.op("pe", MM(bank(7, 512, 64), pk), r=[("ckvT", b), "wuk"], w=[PB(7)])
                        P.op("act", ACOPY(kTp[0:64, bs], bank(7, 512, 64)), w=[PB(7), ("kTn", p, b)])
                    groups.append(gq)
                    groups.append(gk)
                for g in range(2):
                    def gv(g=g):
                        items = []
                        for tt in range(8):
                            t = g * 8 + tt
                            for k in range(2):
                                items.append((ps[:, 7 * 512 + tt * 64:7 * 512 + (tt + 1) * 64], ckvT[:, k, t * 128:(t + 1) * 128],
                                              wuv[:, k, h * 64:(h + 1) * 64], bool(tt == 0 and k == 0), bool(k == 1)))
                        P.op("pe", MMS(items), r=[("ckvT", g * 2), ("ckvT", g * 2 + 1), "wuv"], w=[PB(7)])
                        P.op("dve", VCOPY(vap[:, g * 8:(g + 1) * 8, 0:64], bank(7).rearrange("p (t c) -> p t c", t=8)),
                             w=[PB(7), ("vaug", p, g)])
                    groups.append(gv)
                return groups

            for gfn in proj_groups(0):
                gfn()
            deferred = []
            for h in range(8):
                p = h % 2
                qTp, kTp, vap = qTs[p], kTs[p], vaugs[p]
                pending = proj_groups(h + 1) if h + 1 < 8 else []
                kres = [("kTn", p, b) for b in range(NB)] + [("kTr" if p == 0 else "kTr2", b) for b in range(NB)]
                vres = [("vaug", p, 0), ("vaug", p, 1), "vaug1" if p == 0 else "vaug1b"]
                stg = [(qb, cp) for qb in range(NB) for cp in range(8)]

                def emit_S(i, stage0=stage):
                    qb, cp = stg[i]
                    sbk = ((stage0 + i) % 2) * 2
                    qs = slice(qb * 512, (qb + 1) * 512)
                    items = [(bank(sbk + j), kTp[0:96, (cp * 2 + j) * 128:(cp * 2 + j + 1) * 128], qTp[0:96, qs], True, True)
                             for j in range(2)]
                    P.op("pe", MMS(items), r=kres + [("qTn", p, qb), ("qTr", p, qb)], w=[PB(sbk), PB(sbk + 1)])

                emit_S(0)
                for i, (qb, cp) in enumerate(stg):
                    if i + 1 < len(stg):
                        emit_S(i + 1)
                    sbk = ((stage + i) % 2) * 2
                    st = (stage + i) % 3
                    ob = 4 + qb % 2
                    qs = slice(qb * 512, (qb + 1) * 512)
                    P.op("act", ACT(PT[:, st, :], ps[:, sbk * 512:(sbk + 2) * 512], AF.Exp, scale=scale),
                         w=[PB(sbk), PB(sbk + 1), ("PT", st)])
                    items = []
                    for j in range(2):
                        c = cp * 2 + j
                        items.append((bank(ob), vap[:, c, :], PT[:, st, j * 512:(j + 1) * 512], bool(c == 0), bool(c == NT - 1)))
                    P.op("pe", MMS(items), r=vres + [("PT", st)], w=[PB(ob)])
                    if pending and i >= 2:
                        pending.pop(0)()
                    if cp == 7:
                        r0 = 64 * (h % 2)
                        P.op("dve", RECIPF(rden[64:128, qb % 2, :], bank(ob, 512, 64, 64)), w=[PB(ob), ("rden", qb % 2)])
                        P.op("dve", TT("dve", mixedT[r0:r0 + 64, h // 2, qs], bank(ob, 512, 64), rden[64:128, qb % 2, :], ALU.mult),
                             r=[("rden", qb % 2)], w=[PB(ob), ("mx", h // 2, qb)])
                        while deferred:
                            deferred.pop(0)()
                        deferred.append(lambda h=h, qb=qb: sumsq(mixedT, h, qb, first=(h == 0 and qb == 0), sq=sq))
                while pending:
                    pending.pop(0)()
                stage += len(stg)
            while deferred:
                deferred.pop(0)()
            group_out(0, mixedT, wo)
            P.barrier()
            A.reset(base)
            tabd = A.f32(SEQ)
            qdT = A.bf(SEQ)
            kdT = A.bf(SEQ)
            acc = A.f32(SEQ)
            wdqk = A.bf(KC, 160)
            wdv = A.bf(KC, 256)
            vd = A.bf(3 * NT * 4, 65)
            P.dma(DMA(tabd, tabd_d[:, :]), w=["tabd"], chan="tb")
            P.op("pool", (lambda: nc.gpsimd.memset(vd[:, :, 64:65], 1.0)), w=["vd1"])
            dscale = 0.125
            stage = 0
            mcnt = 0
            pending_epi = []
            allh = [("hT", i) for i in range(NT)]
            for h in range(8):
                g, hh = h // 4, h % 4
                if hh == 0:
                    P.dma(DMA(wdv, wview("wdv")[:, :, g * 256:(g + 1) * 256]), r=[("W", "wdv")], w=["wdv"], chan="dv")
                    vcnt = 0
                    for o, d in enumerate((1, 4, 16)):
                        L = SEQ // d
                        for r in range(d):
                            for c in range(L // 128):
                                tl = (r * L) // 128 + c
                                vb_ = 5 if vcnt % 2 else 7
                                vcnt += 1
                                pairs = []
                                for k in range(KC):
                                    lh = hT[:, k, :].rearrange("p (a b) -> p a b", b=d)[:, 128 * c:128 * c + 128, r]
                                    pairs.append((lh, wdv[:, k, :]))
                                P.op("pe", MM(bank(vb_, 256), pairs), r=allh + ["wdv"], w=[PB(vb_)])
                                src = bank(vb_, 256).rearrange("p (h c) -> p h c", h=4)
                                i0 = (o * NT + tl) * 4
                                if vcnt % 2:
                                    P.op("act", ACOPY(vd[:, i0:i0 + 4, 0:64], src), w=[PB(vb_), ("vd", o)])
                                else:
                                    P.op("dve", VCOPY(vd[:, i0:i0 + 4, 0:64], src), w=[PB(vb_), ("vd", o)])
                P.dma(DMA(wdqk, wview("wdqk")[:, :, h * 160:(h + 1) * 160]), r=[("W", "wdqk")], w=["wdqk"], chan="dq")
                for which, dst, nm in ((0, qdT, "qd"), (1, kdT, "kd")):
                    for b in range(NB):
                        pb_ = 4 if (b + which) % 2 == 0 else 7
                        bs = slice(b * 512, (b + 1) * 512)
                        pairs = [(wdqk[:, k, which * 80:(which + 1) * 80], hT[:, k, bs]) for k in range(KC)]
                        P.op("pe", MM(bank(pb_, 512, 80), pairs), r=[("hT", 4 * b + i) for i in range(4)] + ["wdqk"], w=[PB(pb_)])
                        P.op("act", ACOPY(dst[0:64, bs], bank(pb_, 512, 64)), w=[PB(pb_), (nm, b)])
                        P.op("dve", TT("dve", tmp[0:16, b % 2, :], bank(pb_, 512, 16), tabd[0:16, bs], ALU.mult),
                             r=["tabd"], w=[PB(pb_), ("tmp", b % 2)])
                        P.op("dve", TT("dve", rden[0:16, b % 2, :], bank(pb_, 512, 16, 64), tabd[64:80, bs], ALU.mult),
                             r=["tabd"], w=[PB(pb_), ("rden", b % 2)])
                        P.op("pool", TT("pool", dst[0:16, bs], tmp[0:16, b % 2, :], rden[0:16, b % 2, :], ALU.add),
                             r=[("tmp", b % 2), ("rden", b % 2)], w=[(nm, b)])
                qkres = [("qd", b) for b in range(NB)] + [("kd", b) for b in range(NB)]
                while pending_epi:
                    pending_epi.pop(0)()
                chunks = []
                for o, d in enumerate((1, 4, 16)):
                    L = SEQ // d
                    nch = L // 128
                    for r in range(d):
                        for c in range(nch):
                            lo, hi = max(0, 128 * c - 64), min(L, 128 * c + 192)
                            chunks.append(dict(o=o, d=d, L=L, nch=nch, r=r, c=c, lo=lo, hi=hi, n=hi - lo, mlo=lo - (128 * c - 64)))
                dstg = [chunks[i:i + 2] for i in range(0, len(chunks), 2)]
                started = set()

                def emit_dS(i, stage0=stage):
                    sbk = (stage0 + i) % 3
                    items = []
                    for j, ch in enumerate(dstg[i]):
                        d, r, c = ch["d"], ch["r"], ch["c"]
                        kv = kdT[0:64, :].rearrange("p (a b) -> p a b", b=d)
                        qv = qdT[0:64, :].rearrange("p (a b) -> p a b", b=d)
                        c0 = 256 * j + ch["mlo"]
                        items.append((ps[:, sbk * 512 + c0:sbk * 512 + c0 + ch["n"]], kv[:, 128 * c:128 * c + 128, r],
                                      qv[:, ch["lo"]:ch["hi"], r], True, True))
                    P.op("pe", MMS(items), r=qkres, w=[PB(sbk)])

                emit_dS(0)
                emit_dS(1)
                for i, pair in enumerate(dstg):
                    if i + 2 < len(dstg):
                        emit_dS(i + 2)
                    sbk = (stage + i) % 3
                    st = (stage + i) % 3
                    P.op("act", ACT(PT[:, st, 512:1024], bank(sbk), AF.Exp, scale=dscale), w=[PB(sbk), ("PTe", st)])
                    me = "dve"
                    P.op(me, TT(me, PT[:, st, 0:512], PT[:, st, 512:1024], mask[:, 0:512], ALU.mult),
                         r=[("PTe", st), "mask"], w=[("PT", st)])
                    for j, ch in enumerate(pair):
                        o, d, L, nch, r, c, lo, n = ch["o"], ch["d"], ch["L"], ch["nch"], ch["r"], ch["c"], ch["lo"], ch["n"]
                        c0 = 256 * j + ch["mlo"]
                        tl = (r * L) // 128 + c
                        p_lo = r * L + lo
                        a = p_lo
                        while a < p_lo + n:
                            e = min(p_lo + n, (a // 512 + 1) * 512)
                            bk = a // 512
                            ob = 3 + bk % 2
                            first = (o, bk) not in started
                            started.add((o, bk))
                            i0 = (o * NT + tl) * 4 + hh
                            P.op("pe", MMS([(ps[0:65, ob * 512 + a % 512:ob * 512 + a % 512 + (e - a)], vd[:, i0, :],
                                             PT[:, st, c0 + a - p_lo:c0 + e - p_lo], bool(first), False)]),
                                 r=[("PT", st), ("vd", o), "vd1"], w=[PB(ob)])
                            a = e
                        pend = r * L + min(L, 128 * c + 192)
                        done = []
                        if c == nch - 1:
                            if pend % 512 == 0:
                                done.append(pend // 512 - 1)
                        elif d == 1 and c % 4 == 0 and c > 0:
                            done.append(c // 4 - 1)
                        for bk in done:
                            ob = 3 + bk % 2
                            if d == 1:
                                P.op("dve", VCOPY(acc[0:65, bk * 512:(bk + 1) * 512], bank(ob, 512, 65)), w=[PB(ob), "acc"])
                            elif d == 4:
                                dstv = acc[0:65, :].rearrange("p (a b) -> p a b", b=4)[:, :, bk]
                                P.op("dve", TT("dve", dstv, bank(ob, 512, 65), dstv, ALU.add), w=[PB(ob), "acc"])
                            else:
                                dstv = acc[0:65, :].rearrange("p (a b) -> p b a", b=16)[:, bk * 4:(bk + 1) * 4, :]
                                srcv = bank(ob, 512, 65).rearrange("p (r a) -> p r a", r=4)
                                P.op("dve", TT("dve", dstv, srcv, dstv, ALU.add), w=[PB(ob), "acc"])
                stage += len(dstg)
                accd = acc[64:65, :].rearrange("p (a b) -> p a b", a=4)
                P.op("dve", RECIPF(rden[64:65, :, :], accd[:, 0:2, :]), r=["acc"], w=[("rden", 0), ("rden", 1)])
                P.op("dve", RECIPF(tmp[64:65, :, :], accd[:, 2:4, :]), r=["acc"], w=[("tmp", 0), ("tmp", 1)])

                def epilogue(h=h):
                    r0 = 64 * (h % 2)
                    for b in range(NB):
                        rsrc = (rden if b < 2 else tmp)[64:65, b % 2, :]
                        rres = ("rden" if b < 2 else "tmp", b % 2)
                        bb = 5 if b % 2 == 0 else 7
                        bs = slice(b * 512, (b + 1) * 512)
                        P.op("pe", MM(bank(bb, 512, 64), [(ones_f[64:65, 0:64], rsrc)]), r=[rres, "ones_f"], w=[PB(bb)])
                        P.op("dve", TT("dve", mixedT[r0:r0 + 64, h // 2, bs], acc[0:64, bs], bank(bb, 512, 64), ALU.mult),
                             r=["acc"], w=[PB(bb), ("mx", h // 2, b)])
                        sumsq(mixedT, h, b, first=(h == 0 and b == 0), sq=sq)
                pending_epi.append(epilogue)
            while pending_epi:
                pending_epi.pop(0)()
            group_out(512, mixedT, wo)
            P.barrier()

        xn_big = sb("xn_big", [128, 2, DM], BF16)

        for si in range(nseq):
            ffn(si, "wg1", "wu1", "wd1", final=False)
            if dbg_stage != "ffn":
                mixer(si)
            ffn(si, "wg2", "wu2", "wd2", final=True)
        P.op("sp", lambda: nc.sync.nop(), r=list(out_keys))
        P.emit(stack)
        print("PROG ops", P.stats[0], "sig", P.stats[1], "arena peak", A.peak, flush=True)
    return nc


def _rope_tables():
    pos = np.arange(SEQ, dtype=np.float32)

    def cs(rot):
        half = rot // 2
        inv = np.float32(ROPE_THETA) ** (-np.arange(half, dtype=np.float32) * np.float32(2.0 / rot))
        ang = pos[None, :] * inv.astype(np.float32)[:, None]
        return np.cos(ang).astype(np.float32), np.sin(ang).astype(np.float32)

    cm, sm = cs(32)
    cd, sd = cs(16)
    tabm = np.ones((128, SEQ), np.float32)
    for base in (0, 64):
        tabm[base:base + 16] = cm
        tabm[base + 16:base + 32] = cm
        tabm[base + 32:base + 48] = -sm
        tabm[base + 48:base + 64] = sm
    tabd = np.ones((128, SEQ), np.float32)
    tabd[0:8] = cd
    tabd[8:16] = cd
    tabd[64:72] = -sd
    tabd[72:80] = sd
    return tabm, tabd


def _host_layout(inp):
    g = lambda k: np.asarray(inp[k], np.float32)
    w_in = g("w_in")[0]
    sw32 = np.r_[16:32, 0:16]
    kr = w_in[:, 640:672]
    wlat = np.concatenate([w_in[:, 0:640], kr, kr[:, sw32]], axis=1)
    wuq_src = g("mla_w_uq")[0].reshape(384, 8, 96)
    wuq = np.concatenate([wuq_src, wuq_src[:, :, 64:96][:, :, sw32]], axis=2).reshape(384, 1024)
    wukv = g("mla_w_ukv")[0].reshape(256, 8, 128)
    wuk = wukv[:, :, 0:64].reshape(256, 512)
    wuv = wukv[:, :, 64:128].reshape(256, 512)
    qkv = w_in[:, 672:].reshape(DM, 3, 8, 64)
    sw16 = np.r_[8:16, 0:8]
    qd, kd = qkv[:, 0], qkv[:, 1]
    wdqk = np.concatenate([qd, qd[:, :, sw16], kd, kd[:, :, sw16]], axis=2).reshape(DM, 1280)
    wdv = qkv[:, 2].reshape(DM, 512)

    def fm(v):
        v = np.asarray(v, np.float32).reshape(-1)
        return v.reshape(-1, 128).T

    gains = np.concatenate([fm(g("ffn1_norm")), fm(g("mix_norm")), fm(g("mla_q_norm")), fm(g("mla_kv_norm")),
                            fm(g("mla_out_norm")), fm(g("dil_out_norm")), fm(g("ffn2_norm"))], axis=1)
    assert gains.shape == (128, NG)
    tabm, tabd = _rope_tables()
    jj = np.arange(128)[:, None]
    ii = np.arange(256)[None, :]
    mask = ((ii - jj >= 0) & (ii - jj <= 128)).astype(np.float32)
    mask = np.concatenate([mask, mask], axis=1).astype(ml_dtypes.bfloat16)
    ident = np.eye(128, dtype=np.float32).astype(ml_dtypes.bfloat16)
    c = np.ascontiguousarray
    shared = {
        "wg1": c(g("ffn1_w_gate")[0]), "wu1": c(g("ffn1_w_up")[0]), "wd1": c(g("ffn1_w_down")[0]),
        "wg2": c(g("ffn2_w_gate")[0]), "wu2": c(g("ffn2_w_up")[0]), "wd2": c(g("ffn2_w_down")[0]),
        "wlat": c(wlat), "wuq": c(wuq), "wuk": c(wuk), "wuv": c(wuv), "wdqk": c(wdqk), "wdv": c(wdv),
        "wout": c(g("w_out")[0]), "gains": c(gains), "gfin": c(g("final_norm").reshape(1, DM)),
        "tabm": tabm, "tabd": tabd, "mask": c(mask), "ident": c(ident),
    }
    return shared


def kernel(**inputs):
    x = np.asarray(inputs["x"], np.float32)
    B = x.shape[0]
    nseq = B // N_CORES
    shared = _host_layout(inputs)
    nc = build_program(nseq)
    in_maps = []
    for c in range(N_CORES):
        m = dict(shared)
        m["x"] = np.ascontiguousarray(x[c * nseq:(c + 1) * nseq])
        in_maps.append(m)
    res = run_bass_kernel_spmd(nc, in_maps, core_ids=list(range(N_CORES)))
    out = np.concatenate([np.asarray(r["y"], np.float32) for r in res.results], axis=0)
    return out
```

```python
import math
from contextlib import ExitStack

import numpy as np
import ml_dtypes

import concourse.bass as bass
import concourse.mybir as mybir
from concourse.bass_utils import run_bass_kernel_spmd

F32 = mybir.dt.float32
BF16 = mybir.dt.bfloat16
AF = mybir.ActivationFunctionType
ALU = mybir.AluOpType

N_CORES = 8
SEQ = 2048
DM = 1024
DFF = 2816
NT = SEQ // 128
NB = SEQ // 512
KC = DM // 128
FC = DFF // 128
EPS = 1e-6
ROPE_THETA = 500000.0
USE_FAST_RECIP = False

G1, GMIX, GQ, GKV, GMO, GDO, G2 = 0, 8, 16, 19, 21, 25, 29
NG = 37


class Prog:
    ENGS = ("pe", "act", "dve", "pool", "sp")

    def __init__(self, nc):
        self.nc = nc
        self.E = {"pe": nc.tensor, "act": nc.scalar, "dve": nc.vector, "pool": nc.gpsimd, "sp": nc.sync}
        self.ops = []
        self.last_w = {}
        self.readers = {}
        self.bar = set()
        self.last_eng = {}
        self.last_chan = {}

    def _add(self, eng, fn, r, w, chan=None):
        i = len(self.ops)
        deps = set(self.bar)
        for k in r:
            if k in self.last_w:
                deps.add(self.last_w[k])
        for k in w:
            if k in self.last_w:
                deps.add(self.last_w[k])
            deps.update(self.readers.get(k, ()))
        for k in r:
            self.readers.setdefault(k, []).append(i)
        for k in w:
            self.last_w[k] = i
            self.readers[k] = []
        self.ops.append([eng, fn, deps, chan, None])
        if chan is None:
            self.last_eng[eng] = i
        else:
            self.last_chan[chan] = i
        return i

    def op(self, eng, fn, r=(), w=()):
        return self._add(eng, fn, r, w)

    def dma(self, fn, r=(), w=(), chan=None, q="sp"):
        assert chan is not None
        return self._add(q, fn, r, w, chan)

    def barrier(self):
        self.bar = set(self.last_eng.values()) | set(self.last_chan.values())

    def emit(self, stack):
        nc = self.nc
        ops = self.ops
        has_dep = [False] * len(ops)
        for o in ops:
            for d in o[2]:
                has_dep[d] = True
        cnt = {e: 0 for e in self.ENGS}
        ccnt = {}
        for i, o in enumerate(ops):
            if o[3] is not None:
                ccnt[o[3]] = ccnt.get(o[3], 0) + 16
                o[4] = (("c", o[3]), ccnt[o[3]])
            elif has_dep[i]:
                cnt[o[0]] += 1
                o[4] = (("e", o[0]), cnt[o[0]])
        sems = {}
        for e in self.ENGS:
            if cnt[e]:
                sems[("e", e)] = stack.enter_context(nc.semaphore("s_" + e))
        for c in ccnt:
            sems[("c", c)] = stack.enter_context(nc.semaphore("c_" + c))
        waited = {e: {} for e in self.ENGS}
        for i, o in enumerate(ops):
            eng, fn, deps, chan, sig = o
            need = {}
            for d in deps:
                od = ops[d]
                if od[3] is None and od[0] == eng and eng in ("pe", "sp"):
                    continue
                s, v = od[4]
                if need.get(s, 0) < v:
                    need[s] = v
            E = self.E[eng]
            for s, v in need.items():
                if waited[eng].get(s, 0) >= v:
                    continue
                waited[eng][s] = v
                E.wait_ge(sems[s], v)
            ins = fn()
            if sig is not None:
                ins.then_inc(sems[sig[0]], 16 if chan is not None else 1)
        self.stats = (len(ops), dict(cnt), dict(ccnt))


class Arena:
    def __init__(self, t, n):
        self.t, self.n, self.off = t, n, 0
        self.peak = 0

    def reset(self, off=0):
        self.off = off

    def _take(self, n):
        a = self.off
        self.off += n
        self.peak = max(self.peak, self.off)
        assert self.off <= self.n, ("arena overflow", self.off, self.n)
        return a

    def bf(self, *shape):
        n = int(np.prod(shape))
        a = self._take(n)
        ap = self.t[:, a:a + n]
        if len(shape) == 2:
            return ap.rearrange("p (a b) -> p a b", a=shape[0])
        if len(shape) == 3:
            return ap.rearrange("p (a b c) -> p a b c", a=shape[0], b=shape[1])
        return ap

    def f32(self, *shape):
        self.off = (self.off + 1) // 2 * 2
        n = int(np.prod(shape))
        a = self._take(2 * n)
        ap = self.t[:, a:a + 2 * n].bitcast(F32)
        if len(shape) == 2:
            return ap.rearrange("p (a b) -> p a b", a=shape[0])
        return ap


def build_program(nseq, dbg_stage=None):
    nc = bass.Bass("TRN2", target_bir_lowering=False)
    P = Prog(nc)

    def din(name, shape, dt=F32):
        return nc.dram_tensor(name, list(shape), dt, kind="ExternalInput").ap()

    def dscr(name, shape):
        return nc.dram_tensor(name, list(shape), BF16, kind="Internal").ap()

    x_d = din("x", [nseq, SEQ, DM])
    y_d = nc.dram_tensor("y", [nseq, SEQ, DM], F32, kind="ExternalOutput").ap()
    wsrc = {
        "wg1": din("wg1", [DM, DFF]), "wu1": din("wu1", [DM, DFF]), "wd1": din("wd1", [DFF, DM]),
        "wg2": din("wg2", [DM, DFF]), "wu2": din("wu2", [DM, DFF]), "wd2": din("wd2", [DFF, DM]),
        "wlat": din("wlat", [DM, 704]), "wuq": din("wuq", [384, 1024]),
        "wuk": din("wuk", [256, 512]), "wuv": din("wuv", [256, 512]),
        "wdqk": din("wdqk", [DM, 1280]), "wdv": din("wdv", [DM, 512]), "wout": din("wout", [DM, DM]),
    }
    wgain = {"wg1": G1, "wu1": G1, "wd1": None, "wg2": G2, "wu2": G2, "wd2": None, "wlat": GMIX, "wuq": GQ,
             "wuk": GKV, "wuv": GKV, "wdqk": GMIX, "wdv": GMIX, "wout": GMO}
    wb = {k: dscr(k + "_b", v.shape) for k, v in wsrc.items()}
    gains_d = din("gains", [128, NG])
    gfin_d = din("gfin", [1, DM])
    tabm_d = din("tabm", [128, SEQ])
    tabd_d = din("tabd", [128, SEQ])
    mask_d = din("mask", [128, 512], BF16)
    ident_d = din("ident", [128, 128], BF16)

    stack = ExitStack()
    with stack:
        sb = lambda name, shape, dt: stack.enter_context(nc.sbuf_tensor("sb_" + name, shape, dt))
        xs = sb("xs", [128, NT, DM], F32)
        hT = sb("hT", [128, KC, SEQ], BF16)
        ident = sb("ident", [128, 128], BF16)
        ones_b = sb("ones_b", [128, 8], BF16)
        ones_f = sb("ones_f", [128, 64], F32)
        nhalf = sb("nhalf", [128, 16], F32)
        gains = sb("gains", [128, NG], F32)
        mask = sb("mask", [128, 512], BF16)
        stat = sb("stat", [128, 3, 16], F32)
        rstd_g = sb("rstd_g", [128, 3, 16], F32)
        ARENA_N = 51800
        arena_t = sb("arena", [128, ARENA_N], BF16)
        ps = stack.enter_context(nc.psum_tensor("ps", [128, 8 * 512], F32))
        A = Arena(arena_t, ARENA_N)

        def bank(b, n=512, rows=128, r0=0):
            return ps[r0:r0 + rows, b * 512:b * 512 + n]

        def bank_bf(b, n=1024):
            return ps[:, b * 512:(b + 1) * 512].bitcast(BF16)[:, 0:n]

        PB = lambda b: ("ps", b)

        P.dma(lambda: nc.sync.dma_start(out=ident[:], in_=ident_d[:, :]), w=["ident"], chan="k0")
        P.dma(lambda: nc.sync.dma_start(out=gains[:], in_=gains_d[:, :]), w=["gains"], chan="k1")
        P.dma(lambda: nc.sync.dma_start(out=mask[:], in_=mask_d[:, :]), w=["mask"], chan="k2")
        P.op("pool", lambda: nc.gpsimd.memset(ones_b[:], 1.0), w=["ones_b"])
        P.op("pool", lambda: nc.gpsimd.memset(ones_f[:], 1.0), w=["ones_f"])
        P.op("pool", lambda: nc.gpsimd.memset(nhalf[:], -0.5), w=["nhalf"])

        for t in range(NT):
            P.dma((lambda t=t: nc.sync.dma_start(out=xs[:, t, :], in_=x_d[0, t * 128:(t + 1) * 128, :])), w=[("xs", t)], chan="xl%d" % t)
        A.reset()
        stg_f = A.f32(4, DFF)
        stg_b = A.bf(4, DFF)
        it = 0
        for name, src in wsrc.items():
            K, N = src.shape
            for j in range(K // 128):
                s = it % 4
                gcol = wgain[name]
                if name == "wout" and j >= 4:
                    gcol = GDO - 4
                P.dma(lambda s=s, src=src, j=j, N=N: nc.sync.dma_start(out=stg_f[:, s, 0:N], in_=src[j * 128:(j + 1) * 128, :]),
                      w=[("stf", s)], chan="il%d" % s)
                if gcol is None:
                    if it % 2 == 0:
                        P.op("act", lambda s=s, N=N: nc.scalar.copy(out=stg_b[:, s, 0:N], in_=stg_f[:, s, 0:N]),
                             r=[("stf", s)], w=[("stb", s)])
                    else:
                        P.op("dve", lambda s=s, N=N: nc.vector.tensor_copy(out=stg_b[:, s, 0:N], in_=stg_f[:, s, 0:N]),
                             r=[("stf", s)], w=[("stb", s)])
                else:
                    gc = gcol + j
                    if it % 2 == 0:
                        P.op("act", lambda s=s, N=N, gc=gc: nc.scalar.activation(out=stg_b[:, s, 0:N], in_=stg_f[:, s, 0:N],
                                                                                 func=AF.Copy, scale=gains[:, gc:gc + 1]),
                             r=[("stf", s), "gains"], w=[("stb", s)])
                    else:
                        P.op("dve", lambda s=s, N=N, gc=gc: nc.vector.tensor_scalar(out=stg_b[:, s, 0:N], in0=stg_f[:, s, 0:N],
                                                                                    scalar1=gains[:, gc:gc + 1], scalar2=None,
                                                                                    op0=ALU.mult),
                             r=[("stf", s), "gains"], w=[("stb", s)])
                P.dma(lambda s=s, name=name, j=j, N=N: nc.sync.dma_start(out=wb[name][j * 128:(j + 1) * 128, :], in_=stg_b[:, s, 0:N]),
                      r=[("stb", s)], w=[("W", name)], chan="is%d" % s)
                it += 1
        P.barrier()

        def MM(out, pairs, start=True, skip=False):
            def fn():
                ins = None
                n = len(pairs)
                for i, (l, r) in enumerate(pairs):
                    ins = nc.tensor.matmul(out, lhsT=l, rhs=r, start=(start and i == 0), stop=(i == n - 1),
                                           skip_group_check=skip)
                return ins
            return fn

        def MMS(items):
            def fn():
                ins = None
                for (o, l, r, st, sp) in items:
                    ins = nc.tensor.matmul(o, lhsT=l, rhs=r, start=st, stop=sp, skip_group_check=True)
                return ins
            return fn

        def ACT(out, in_, func, **kw):
            return lambda: nc.scalar.activation(out=out, in_=in_, func=func, **kw)

        def ACOPY(out, in_):
            return lambda: nc.scalar.copy(out=out, in_=in_)

        def TT(eng, out, in0, in1, op):
            e = nc.vector if eng == "dve" else nc.gpsimd
            return lambda: e.tensor_tensor(out=out, in0=in0, in1=in1, op=op)

        def TS(out, in0, s1, s2, op0, op1=None):
            if op1 is None:
                return lambda: nc.vector.tensor_scalar(out=out, in0=in0, scalar1=s1, scalar2=None, op0=op0)
            return lambda: nc.vector.tensor_scalar(out=out, in0=in0, scalar1=s1, scalar2=s2, op0=op0, op1=op1)

        def STT(out, in0, scalar, in1, op0, op1):
            return lambda: nc.vector.scalar_tensor_tensor(out=out, in0=in0, scalar=scalar, in1=in1, op0=op0, op1=op1)

        def VCOPY(out, in_):
            return lambda: nc.vector.tensor_copy(out=out, in_=in_)

        def RECIP(out, in_):
            return lambda: nc.vector.reciprocal(out=out, in_=in_)

        def RECIPF(out, in_):
            if USE_FAST_RECIP:
                return lambda: nc.vector.reciprocal_approx_fast(out=out, in_=in_)
            return lambda: nc.vector.reciprocal(out=out, in_=in_)

        def DMA(out, in_):
            return lambda: nc.sync.dma_start(out=out, in_=in_)

        statc = [0]

        def rms_rstd(src_ap, n, junk_ap, src_res, junk_res):
            c = statc[0] % 16
            statc[0] += 1
            P.op("act", ACT(junk_ap, src_ap, AF.Square, accum_out=stat[:, 0, c:c + 1]), r=src_res, w=junk_res + [("st0", c)])
            P.op("dve", TS(stat[:, 1, c:c + 1], stat[:, 0, c:c + 1], 1.0 / n, EPS, ALU.mult, ALU.add), r=[("st0", c)], w=[("st1", c)])
            P.op("pool", TT("pool", stat[:, 2, c:c + 1], stat[:, 1, c:c + 1], nhalf[:, 0:1], ALU.pow),
                 r=[("st1", c), "nhalf"], w=[("st2", c)])
            return stat[:, 2, c:c + 1], ("st2", c)

        tb = [0]

        def norm_T(src_ap, n, src_res, xn_ap, xn_res, dstT, dst_res, tok0, evac_eng):
            nch = n // 128
            rstd, rres = rms_rstd(src_ap, n, xn_ap, src_res, xn_res)
            P.op("dve", TS(xn_ap, src_ap, rstd, None, ALU.mult), r=src_res + [rres], w=xn_res)
            b = tb[0] % 2
            tb[0] += 1
            pst = bank_bf(b)

            def tr():
                ins = None
                for j in range(nch):
                    ins = nc.tensor.transpose(out=pst[:, j * 128:(j + 1) * 128], in_=xn_ap[:, j * 128:(j + 1) * 128],
                                              identity=ident[:])
                return ins
            P.op("pe", tr, r=xn_res + ["ident"], w=[PB(b)])
            src3 = pst[:, 0:n].rearrange("p (j c) -> p j c", j=nch)
            dst3 = dstT[:, 0:nch, tok0:tok0 + 128]
            if evac_eng == "act":
                P.op("act", ACOPY(dst3, src3), w=[PB(b)] + dst_res)
            else:
                P.op("dve", VCOPY(dst3, src3), w=[PB(b)] + dst_res)

        def wview(name):
            return wb[name].rearrange("(c p) n -> p c n", p=128)

        out_keys = []
        pre_mixer_norm = (dbg_stage != "ffn")

        def mixer_norm(t):
            norm_T(xs[:, t, :], DM, [("xs", t)], xn_big[:, t % 2, :], [("xnb", t % 2)], hT, [("hT", t)], t * 128, "act")

        def load_x(si, t):
            P.dma(DMA(xs[:, t, :], x_d[si, t * 128:(t + 1) * 128, :]), w=[("xs", t)], chan="xl%d" % t)

        def ffn(si, wg, wu, wd, final):
            A.reset()
            aT = A.bf(FC, 512)
            wgu = A.bf(2, 2, KC * 256).rearrange("p s g (k c) -> p s g k c", k=KC)
            wdt = A.bf(FC, DM)
            sg = A.f32(2, 512)
            gfin = A.f32(DM) if final else None
            if final:
                P.dma(DMA(gfin, gfin_d.to_broadcast([128, DM])), w=["gfin"], chan="gf")
            wgv, wuv_, wdv_ = wview(wg), wview(wu), wview(wd)
            NP = FC // 2

            def load_gu(gi):
                s = gi % 2
                c0 = (gi % NP) * 256
                P.dma(DMA(wgu[:, s, 0, :, :], wgv[:, :, c0:c0 + 256]), r=[("W", wg)], w=[("wgu", s, 0)], chan="gu%d" % s)
                P.dma(DMA(wgu[:, s, 1, :, :], wuv_[:, :, c0:c0 + 256]), r=[("W", wu)], w=[("wgu", s, 1)], chan="gu%d" % s)

            load_gu(0)
            gi = 0
            def ffn_norm(t):
                norm_T(xs[:, t, :], DM, [("xs", t)], xn_big[:, t % 2, :], [("xnb", t % 2)], hT, [("hT", t)], t * 128, "act")

            for tt in range(4):
                ffn_norm(tt)
            for b in range(NB):
                for h2 in range(2):
                    P.dma(DMA(wdt[:, h2 * 11:(h2 + 1) * 11, :], wdv_[:, h2 * 11:(h2 + 1) * 11, :]),
                          r=[("W", wd)], w=[("wdt", h2)], chan="wd%d" % h2)
                hres = [("hT", b * 4 + tt) for tt in range(4)]
                for fp in range(NP):
                    if gi + 1 < NB * NP:
                        load_gu(gi + 1)
                    s = gi % 2
                    for fi in range(2):
                        f = fp * 2 + fi
                        gb, ub = 2 + f % 2, 4 + f % 2
                        for g, dstb in ((0, gb), (1, ub)):
                            pairs = [(wgu[:, s, g, k, fi * 128:(fi + 1) * 128], hT[:, k, b * 512:(b + 1) * 512]) for k in range(KC)]
                            P.op("pe", MM(bank(dstb), pairs), r=hres + [("wgu", s, 0), ("wgu", s, 1)], w=[PB(dstb)])
                        P.op("act", ACT(sg[:, f % 2, :], bank(gb), AF.Silu), w=[PB(gb), ("sg", f % 2)])
                        P.op("dve", TT("dve", aT[:, f, :], bank(ub), sg[:, f % 2, :], ALU.mult), r=[("sg", f % 2)], w=[PB(ub), ("aT", f)])
                    gi += 1
                ares = [("aT", f) for f in range(FC)]
                for tt in range(4):
                    t = b * 4 + tt
                    for hf in range(2):
                        db = 6 + (tt * 2 + hf) % 2
                        pairs = [(aT[:, f, tt * 128:(tt + 1) * 128], wdt[:, f, hf * 512:(hf + 1) * 512]) for f in range(FC)]
                        P.op("pe", MM(bank(db), pairs), r=ares + [("wdt", 0), ("wdt", 1)], w=[PB(db)])
                        xsl = xs[:, t, hf * 512:(hf + 1) * 512]
                        P.op("dve", STT(xsl, bank(db), 0.5, xsl, ALU.mult, ALU.add), w=[PB(db), ("xs", t)])
                    if b + 1 < NB:
                        ffn_norm((b + 1) * 4 + tt)
                    elif not final and pre_mixer_norm:
                        for t2 in range(3 * tt, 3 * tt + 3):
                            mixer_norm(t2)
                    if final:
                        rstd, rres = rms_rstd(xs[:, t, :], DM, xn_big[:, tt % 2, :], [("xs", t)], [("xnb", tt % 2)])
                        P.op("dve", STT(xs[:, t, :], xs[:, t, :], rstd, gfin, ALU.mult, ALU.mult), r=[rres, "gfin"], w=[("xs", t)])
                        P.dma(DMA(y_d[si, t * 128:(t + 1) * 128, :], xs[:, t, :]), r=[("xs", t)], w=[("y", si, t)], chan="yst%d" % t)
                        out_keys.append(("y", si, t))
                        if si + 1 < nseq:
                            load_x(si + 1, t)
            P.barrier()

        def group_out(wrow0, mixedT, wo):
            P.op("dve", TS(rstd_g[:, 1, :], bank(6, 16), 1.0 / 512, EPS, ALU.mult, ALU.add), w=[PB(6), "rg1"])
            P.op("pool", TT("pool", rstd_g[:, 2, :], rstd_g[:, 1, :], nhalf[:, 0:16], ALU.pow), r=["rg1", "nhalf"], w=["rg2"])
            wv = wview("wout")
            c0 = wrow0 // 128
            for h2 in range(2):
                P.dma(DMA(wo[:, h2 * 2:(h2 + 1) * 2, :], wv[:, c0 + h2 * 2:c0 + (h2 + 1) * 2, :]),
                      r=[("W", "wout")], w=[("wo", h2)], chan="wo%d" % h2)
            mres = [("mx", j, b) for j in range(4) for b in range(NB)]
            for t in range(NT):
                for hf in range(2):
                    db = (t * 2 + hf) % 2
                    pairs = [(mixedT[:, j, t * 128:(t + 1) * 128], wo[:, j, hf * 512:(hf + 1) * 512]) for j in range(4)]
                    P.op("pe", MM(bank(db), pairs), r=mres + [("wo", 0), ("wo", 1)], w=[PB(db)])
                    xsl = xs[:, t, hf * 512:(hf + 1) * 512]
                    P.op("dve", STT(xsl, bank(db), rstd_g[:, 2, t:t + 1], xsl, ALU.mult, ALU.add), r=["rg2"], w=[PB(db), ("xs", t)])

        def sumsq_sq(mixedT, h, b, sq):
            r0 = 64 * (h % 2)
            j = h // 2
            msl = mixedT[r0:r0 + 64, j, b * 512:(b + 1) * 512]
            P.op("pool", TT("pool", sq[r0:r0 + 64, b % 2, :], msl, msl, ALU.mult), r=[("mx", j, b)], w=[("sq", b % 2)])

        def sumsq_mm(h, b, first, sq):
            r0 = 64 * (h % 2)
            items = []
            for tt in range(4):
                t = b * 4 + tt
                items.append((ps[:, 6 * 512 + t:6 * 512 + t + 1], sq[r0:r0 + 64, b % 2, tt * 128:(tt + 1) * 128],
                              ones_b[r0:r0 + 64, 0:1], bool(first and tt == 0), False))
            P.op("pe", MMS(items), r=[("sq", b % 2), "ones_b"], w=[PB(6)])

        def mixer(si):
            A.reset()
            mixedT = A.bf(4, SEQ)
            wo = A.bf(4, DM)
            PT = A.bf(3, 1024)
            tmp = A.f32(2, 512)
            rden = A.f32(2, 512)
            sq = A.bf(2, 512)
            base = A.off
            tabm = A.f32(SEQ)
            cqT = A.bf(3, SEQ)
            ckvT = A.bf(2, SEQ)
            kT = A.bf(SEQ)
            qT = A.bf(SEQ)
            vaug = A.bf(NT, 128)
            wuq = A.bf(3, 1024)
            wuk = A.bf(2, 512)
            wuv = A.bf(2, 512)
            wlat = A.bf(KC, 704)
            P.dma(DMA(tabm, tabm_d[:, :]), w=["tabm"], chan="tb")
            P.dma(DMA(wlat, wview("wlat")), r=[("W", "wlat")], w=["wlat"], chan="wl")
            P.dma(DMA(wuq, wview("wuq")), r=[("W", "wuq")], w=["wuq"], chan="wq")
            P.dma(DMA(wuk, wview("wuk")), r=[("W", "wuk")], w=["wuk"], chan="wk")
            P.dma(DMA(wuv, wview("wuv")), r=[("W", "wuv")], w=["wuv"], chan="wv")
            P.op("pool", (lambda: nc.gpsimd.memset(vaug[:, :, 64:128], 1.0)), w=["vaug1"])
            for t in range(12 if pre_mixer_norm else 0, NT):
                mixer_norm(t)
            for t in range(NT):
                qb_, kb_ = 2 + t % 2, 4 + t % 2
                pq = [(hT[:, k, t * 128:(t + 1) * 128], wlat[:, k, 0:384]) for k in range(KC)]
                pk = [(hT[:, k, t * 128:(t + 1) * 128], wlat[:, k, 384:640]) for k in range(KC)]
                P.op("pe", MM(bank(qb_, 384), pq), r=[("hT", t), "wlat"], w=[PB(qb_)])
                P.op("pe", MM(bank(kb_, 256), pk), r=[("hT", t), "wlat"], w=[PB(kb_)])
                norm_T(bank(qb_, 384), 384, [PB(qb_)], xn_big[:, 0, 0:384], [("xnb", 0)], cqT, [("cqT", t // 4)], t * 128, "dve")
                norm_T(bank(kb_, 256), 256, [PB(kb_)], xn_big[:, 1, 0:256], [("xnb", 1)], ckvT, [("ckvT", t // 4)], t * 128, "dve")
            for b in range(NB):
                kb_ = 6 + b % 2
                pr = [(wlat[:, k, 640:704], hT[:, k, b * 512:(b + 1) * 512]) for k in range(KC)]
                P.op("pe", MM(bank(kb_, 512, 64), pr), r=[("hT", 4 * b + i) for i in range(4)] + ["wlat"], w=[PB(kb_)])
                P.op("dve", TT("dve", tmp[0:32, b % 2, :], bank(kb_, 512, 32), tabm[0:32, b * 512:(b + 1) * 512], ALU.mult),
                     r=["tabm"], w=[PB(kb_), ("tmp", b % 2)])
                P.op("dve", TT("dve", rden[0:32, b % 2, :], bank(kb_, 512, 32, 32), tabm[32:64, b * 512:(b + 1) * 512], ALU.mult),
                     r=["tabm"], w=[PB(kb_), ("rden", b % 2)])
                P.op("pool", TT("pool", kT[64:96, b * 512:(b + 1) * 512], tmp[0:32, b % 2, :], rden[0:32, b % 2, :], ALU.add),
                     r=[("tmp", b % 2), ("rden", b % 2)], w=[("kTr", b)])
            P.barrier()
            A.reset(A.off - KC * 704)
            kT2 = A.bf(SEQ)
            vaug2 = A.bf(NT, 128)
            qT2 = xn_big.rearrange("p a b -> p (a b)")
            qTs, kTs, vaugs = [qT, qT2], [kT, kT2], [vaug, vaug2]
            P.op("pool", (lambda: nc.gpsimd.memset(vaug2[:, :, 64:128], 1.0)), w=["vaug1b"])
            P.op("pool", (lambda: nc.gpsimd.tensor_copy(out=kT2[64:96, :], in_=kT[64:96, :])),
                 r=[("kTr", b) for b in range(NB)], w=[("kTr2", b) for b in range(NB)])
            scale = 96 ** -0.5
            stage = 0

            def proj_groups(h):
                p = h % 2
                qTp, kTp, vap = qTs[p], kTs[p], vaugs[p]
                groups = []
                for b in range(NB):
                    bs = slice(b * 512, (b + 1) * 512)

                    def gq(b=b, bs=bs):
                        pq = [(wuq[:, k, h * 128:(h + 1) * 128], cqT[:, k, bs]) for k in range(3)]
                        P.op("pe", MM(bank(7), pq), r=[("cqT", b), "wuq"], w=[PB(7)])
                        P.op("act", ACOPY(qTp[0:64, bs], bank(7, 512, 64)), w=[PB(7), ("qTn", p, b)])
                        P.op("dve", TT("dve", tmp[64:96, 0, :], bank(7, 512, 32, 64), tabm[64:96, bs], ALU.mult),
                             r=["tabm"], w=[PB(7), ("tmp", 0)])
                        P.op("dve", TT("dve", tmp[64:96, 1, :], bank(7, 512, 32, 96), tabm[96:128, bs], ALU.mult),
                             r=["tabm"], w=[PB(7), ("tmp", 1)])
                        P.op("dve", TT("dve", qTp[64:96, bs], tmp[64:96, 0, :], tmp[64:96, 1, :], ALU.add),
                             r=[("tmp", 0), ("tmp", 1)], w=[("qTr", p, b)])

                    def gk(b=b, bs=bs):
                        pk = [(wuk[:, k, h * 64:(h + 1) * 64], ckvT[:, k, bs]) for k in range(2)]
                        P.op("pe", MM(bank(7, 512, 64), pk), r=[("ckvT", b), "wuk"], w=[PB(7)])
                        P.op("act", ACOPY(kTp[0:64, bs], bank(7, 512, 64)), w=[PB(7), ("kTn", p, b)])
                    groups.append(gq)
                    groups.append(gk)
                for g in range(2):
                    def gv(g=g):
                        items = []
                        for tt in range(8):
                            t = g * 8 + tt
                            for k in range(2):
                                items.append((ps[:, 7 * 512 + tt * 64:7 * 512 + (tt + 1) * 64], ckvT[:, k, t * 128:(t + 1) * 128],
                                              wuv[:, k, h * 64:(h + 1) * 64], bool(tt == 0 and k == 0), bool(k == 1)))
                        P.op("pe", MMS(items), r=[("ckvT", g * 2), ("ckvT", g * 2 + 1), "wuv"], w=[PB(7)])
                        P.op("dve", VCOPY(vap[:, g * 8:(g + 1) * 8, 0:64], bank(7).rearrange("p (t c) -> p t c", t=8)),
                             w=[PB(7), ("vaug", p, g)])
                    groups.append(gv)
                return groups

            for gfn in proj_groups(0):
                gfn()
            deferred = []
            for h in range(8):
                p = h % 2
                qTp, kTp, vap = qTs[p], kTs[p], vaugs[p]
                pending = proj_groups(h + 1) if h + 1 < 8 else []
                kres = [("kTn", p, b) for b in range(NB)] + [("kTr" if p == 0 else "kTr2", b) for b in range(NB)]
                vres = [("vaug", p, 0), ("vaug", p, 1), "vaug1" if p == 0 else "vaug1b"]
                stg = [(qb, cp) for qb in range(NB) for cp in range(8)]

                def emit_S(i, stage0=stage):
                    qb, cp = stg[i]
                    sbk = ((stage0 + i) % 2) * 2
                    qs = slice(qb * 512, (qb + 1) * 512)
                    items = [(bank(sbk + j), kTp[0:96, (cp * 2 + j) * 128:(cp * 2 + j + 1) * 128], qTp[0:96, qs], True, True)
                             for j in range(2)]
                    P.op("pe", MMS(items), r=kres + [("qTn", p, qb), ("qTr", p, qb)], w=[PB(sbk), PB(sbk + 1)])

                emit_S(0)
                for i, (qb, cp) in enumerate(stg):
                    if i + 1 < len(stg):
                        emit_S(i + 1)
                    sbk = ((stage + i) % 2) * 2
                    st = (stage + i) % 3
                    ob = 4 + qb % 2
                    qs = slice(qb * 512, (qb + 1) * 512)
                    P.op("act", ACT(PT[:, st, :], ps[:, sbk * 512:(sbk + 2) * 512], AF.Exp, scale=scale),
                         w=[PB(sbk), PB(sbk + 1), ("PT", st)])
                    items = []
                    for j in range(2):
                        c = cp * 2 + j
                        items.append((bank(ob), vap[:, c, :], PT[:, st, j * 512:(j + 1) * 512], bool(c == 0), bool(c == NT - 1)))
                    P.op("pe", MMS(items), r=vres + [("PT", st)], w=[PB(ob)])
                    if pending and i >= 2:
                        pending.pop(0)()
                    if cp == 7:
                        r0 = 64 * (h % 2)
                        P.op("dve", RECIPF(rden[64:128, qb % 2, :], bank(ob, 512, 64, 64)), w=[PB(ob), ("rden", qb % 2)])
                        P.op("dve", TT("dve", mixedT[r0:r0 + 64, h // 2, qs], bank(ob, 512, 64), rden[64:128, qb % 2, :], ALU.mult),
                             r=[("rden", qb % 2)], w=[PB(ob), ("mx", h // 2, qb)])
                        while deferred:
                            deferred.pop(0)()
                        sumsq_sq(mixedT, h, qb, sq)
                        deferred.append(lambda h=h, qb=qb: sumsq_mm(h, qb, first=(h == 0 and qb == 0), sq=sq))
                while pending:
                    pending.pop(0)()
                stage += len(stg)
            while deferred:
                deferred.pop(0)()
            group_out(0, mixedT, wo)
            P.barrier()
            A.reset(base)
            tabd = A.f32(SEQ)
            qdT = A.bf(SEQ)
            kdT = A.bf(SEQ)
            acc = A.f32(SEQ)
            wdqk = A.bf(KC, 160)
            wdv = A.bf(KC, 256)
            vd = A.bf(3 * NT * 4, 65)
            P.dma(DMA(tabd, tabd_d[:, :]), w=["tabd"], chan="tb")
            P.op("pool", (lambda: nc.gpsimd.memset(vd[:, :, 64:65], 1.0)), w=["vd1"])
            dscale = 0.125
            stage = 0
            mcnt = 0
            pending_epi = []
            OB4 = (3, 4, 5, 7)
            allh = [("hT", i) for i in range(NT)]
            for h in range(8):
                g, hh = h // 4, h % 4
                if hh == 0:
                    P.dma(DMA(wdv, wview("wdv")[:, :, g * 256:(g + 1) * 256]), r=[("W", "wdv")], w=["wdv"], chan="dv")
                    vcnt = 0
                    for o, d in enumerate((1, 4, 16)):
                        L = SEQ // d
                        for r in range(d):
                            for c in range(L // 128):
                                tl = (r * L) // 128 + c
                                vb_ = 5 if vcnt % 2 else 7
                                vcnt += 1
                                pairs = []
                                for k in range(KC):
                                    lh = hT[:, k, :].rearrange("p (a b) -> p a b", b=d)[:, 128 * c:128 * c + 128, r]
                                    pairs.append((lh, wdv[:, k, :]))
                                P.op("pe", MM(bank(vb_, 256), pairs), r=allh + ["wdv"], w=[PB(vb_)])
                                src = bank(vb_, 256).rearrange("p (h c) -> p h c", h=4)
                                i0 = (o * NT + tl) * 4
                                if vcnt % 2:
                                    P.op("act", ACOPY(vd[:, i0:i0 + 4, 0:64], src), w=[PB(vb_), ("vd", o)])
                                else:
                                    P.op("dve", VCOPY(vd[:, i0:i0 + 4, 0:64], src), w=[PB(vb_), ("vd", o)])
                P.dma(DMA(wdqk, wview("wdqk")[:, :, h * 160:(h + 1) * 160]), r=[("W", "wdqk")], w=["wdqk"], chan="dq")
                for which, dst, nm in ((0, qdT, "qd"), (1, kdT, "kd")):
                    for b in range(NB):
                        pb_ = 4 if (b + which) % 2 == 0 else 7
                        bs = slice(b * 512, (b + 1) * 512)
                        pairs = [(wdqk[:, k, which * 80:(which + 1) * 80], hT[:, k, bs]) for k in range(KC)]
                        P.op("pe", MM(bank(pb_, 512, 80), pairs), r=[("hT", 4 * b + i) for i in range(4)] + ["wdqk"], w=[PB(pb_)])
                        P.op("act", ACOPY(dst[0:64, bs], bank(pb_, 512, 64)), w=[PB(pb_), (nm, b)])
                        P.op("dve", TT("dve", tmp[0:16, b % 2, :], bank(pb_, 512, 16), tabd[0:16, bs], ALU.mult),
                             r=["tabd"], w=[PB(pb_), ("tmp", b % 2)])
                        P.op("dve", TT("dve", rden[0:16, b % 2, :], bank(pb_, 512, 16, 64), tabd[64:80, bs], ALU.mult),
                             r=["tabd"], w=[PB(pb_), ("rden", b % 2)])
                        P.op("pool", TT("pool", dst[0:16, bs], tmp[0:16, b % 2, :], rden[0:16, b % 2, :], ALU.add),
                             r=[("tmp", b % 2), ("rden", b % 2)], w=[(nm, b)])
                qkres = [("qd", b) for b in range(NB)] + [("kd", b) for b in range(NB)]
                while pending_epi:
                    pending_epi.pop(0)()
                chunks = []
                for o, d in enumerate((1, 4, 16)):
                    L = SEQ // d
                    nch = L // 128
                    for r in range(d):
                        for c in range(nch):
                            lo, hi = max(0, 128 * c - 64), min(L, 128 * c + 192)
                            chunks.append(dict(o=o, d=d, L=L, nch=nch, r=r, c=c, lo=lo, hi=hi, n=hi - lo, mlo=lo - (128 * c - 64)))
                dstg = [chunks[i:i + 2] for i in range(0, len(chunks), 2)]
                started = set()

                def emit_dS(i, stage0=stage):
                    sbk = (stage0 + i) % 3
                    items = []
                    for j, ch in enumerate(dstg[i]):
                        d, r, c = ch["d"], ch["r"], ch["c"]
                        kv = kdT[0:64, :].rearrange("p (a b) -> p a b", b=d)
                        qv = qdT[0:64, :].rearrange("p (a b) -> p a b", b=d)
                        c0 = 256 * j + ch["mlo"]
                        items.append((ps[:, sbk * 512 + c0:sbk * 512 + c0 + ch["n"]], kv[:, 128 * c:128 * c + 128, r],
                                      qv[:, ch["lo"]:ch["hi"], r], True, True))
                    P.op("pe", MMS(items), r=qkres, w=[PB(sbk)])

                emit_dS(0)
                emit_dS(1)
                for i, pair in enumerate(dstg):
                    if i + 2 < len(dstg):
                        emit_dS(i + 2)
                    sbk = (stage + i) % 3
                    st = (stage + i) % 3
                    P.op("act", ACT(PT[:, st, 512:1024], bank(sbk), AF.Exp, scale=dscale), w=[PB(sbk), ("PTe", st)])
                    me = "dve"
                    P.op(me, TT(me, PT[:, st, 0:512], PT[:, st, 512:1024], mask[:, 0:512], ALU.mult),
                         r=[("PTe", st), "mask"], w=[("PT", st)])
                    for j, ch in enumerate(pair):
                        o, d, L, nch, r, c, lo, n = ch["o"], ch["d"], ch["L"], ch["nch"], ch["r"], ch["c"], ch["lo"], ch["n"]
                        c0 = 256 * j + ch["mlo"]
                        tl = (r * L) // 128 + c
                        p_lo = r * L + lo
                        a = p_lo
                        while a < p_lo + n:
                            e = min(p_lo + n, (a // 512 + 1) * 512)
                            bk = a // 512
                            ob = OB4[bk % 4]
                            first = (o, bk) not in started
                            started.add((o, bk))
                            i0 = (o * NT + tl) * 4 + hh
                            P.op("pe", MMS([(ps[0:65, ob * 512 + a % 512:ob * 512 + a % 512 + (e - a)], vd[:, i0, :],
                                             PT[:, st, c0 + a - p_lo:c0 + e - p_lo], bool(first), False)]),
                                 r=[("PT", st), ("vd", o), "vd1"], w=[PB(ob)])
                            a = e
                        pend = r * L + min(L, 128 * c + 192)
                        done = []
                        if c == nch - 1:
                            if pend % 512 == 0:
                                done.append(pend // 512 - 1)
                        elif d == 1 and c % 4 == 0 and c > 0:
                            done.append(c // 4 - 1)
                        for bk in done:
                            ob = OB4[bk % 4]
                            if d == 1:
                                P.op("dve", VCOPY(acc[0:65, bk * 512:(bk + 1) * 512], bank(ob, 512, 65)), w=[PB(ob), "acc"])
                            elif d == 4:
                                dstv = acc[0:65, :].rearrange("p (a b) -> p a b", b=4)[:, :, bk]
                                P.op("dve", TT("dve", dstv, bank(ob, 512, 65), dstv, ALU.add), w=[PB(ob), "acc"])
                            else:
                                dstv = acc[0:65, :].rearrange("p (a b) -> p b a", b=16)[:, bk * 4:(bk + 1) * 4, :]
                                srcv = bank(ob, 512, 65).rearrange("p (r a) -> p r a", r=4)
                                P.op("dve", TT("dve", dstv, srcv, dstv, ALU.add), w=[PB(ob), "acc"])
                stage += len(dstg)
                accd = acc[64:65, :].rearrange("p (a b) -> p a b", a=4)
                P.op("dve", RECIPF(rden[64:65, :, :], accd[:, 0:2, :]), r=["acc"], w=[("rden", 0), ("rden", 1)])
                P.op("dve", RECIPF(tmp[64:65, :, :], accd[:, 2:4, :]), r=["acc"], w=[("tmp", 0), ("tmp", 1)])

                def epilogue(h=h):
                    r0 = 64 * (h % 2)
                    for b in range(NB):
                        rsrc = (rden if b < 2 else tmp)[64:65, b % 2, :]
                        rres = ("rden" if b < 2 else "tmp", b % 2)
                        bb = 5 if b % 2 == 0 else 7
                        bs = slice(b * 512, (b + 1) * 512)
                        P.op("pe", MM(bank(bb, 512, 64), [(ones_f[64:65, 0:64], rsrc)]), r=[rres, "ones_f"], w=[PB(bb)])
                        P.op("dve", TT("dve", mixedT[r0:r0 + 64, h // 2, bs], acc[0:64, bs], bank(bb, 512, 64), ALU.mult),
                             r=["acc"], w=[PB(bb), ("mx", h // 2, b)])
                        sumsq_sq(mixedT, h, b, sq)
                        if b >= 1:
                            sumsq_mm(h, b - 1, first=(h == 0 and b == 1), sq=sq)
                    sumsq_mm(h, NB - 1, first=False, sq=sq)
                pending_epi.append(epilogue)
            while pending_epi:
                pending_epi.pop(0)()
            group_out(512, mixedT, wo)
            P.barrier()

        xn_big = sb("xn_big", [128, 2, DM], BF16)

        for si in range(nseq):
            ffn(si, "wg1", "wu1", "wd1", final=False)
            if dbg_stage != "ffn":
                mixer(si)
            ffn(si, "wg2", "wu2", "wd2", final=True)
        P.op("sp", lambda: nc.sync.nop(), r=list(out_keys))
        P.emit(stack)
        print("PROG ops", P.stats[0], "sig", P.stats[1], "arena peak", A.peak, flush=True)
    return nc


def _rope_tables():
    pos = np.arange(SEQ, dtype=np.float32)

    def cs(rot):
        half = rot // 2
        inv = np.float32(ROPE_THETA) ** (-np.arange(half, dtype=np.float32) * np.float32(2.0 / rot))
        ang = pos[None, :] * inv.astype(np.float32)[:, None]
        return np.cos(ang).astype(np.float32), np.sin(ang).astype(np.float32)

    cm, sm = cs(32)
    cd, sd = cs(16)
    tabm = np.ones((128, SEQ), np.float32)
    for base in (0, 64):
        tabm[base:base + 16] = cm
        tabm[base + 16:base + 32] = cm
        tabm[base + 32:base + 48] = -sm
        tabm[base + 48:base + 64] = sm
    tabd = np.ones((128, SEQ), np.float32)
    tabd[0:8] = cd
    tabd[8:16] = cd
    tabd[64:72] = -sd
    tabd[72:80] = sd
    return tabm, tabd


def _host_layout(inp):
    g = lambda k: np.asarray(inp[k], np.float32)
    w_in = g("w_in")[0]
    sw32 = np.r_[16:32, 0:16]
    kr = w_in[:, 640:672]
    wlat = np.concatenate([w_in[:, 0:640], kr, kr[:, sw32]], axis=1)
    wuq_src = g("mla_w_uq")[0].reshape(384, 8, 96)
    wuq = np.concatenate([wuq_src, wuq_src[:, :, 64:96][:, :, sw32]], axis=2).reshape(384, 1024)
    wukv = g("mla_w_ukv")[0].reshape(256, 8, 128)
    wuk = wukv[:, :, 0:64].reshape(256, 512)
    wuv = wukv[:, :, 64:128].reshape(256, 512)
    qkv = w_in[:, 672:].reshape(DM, 3, 8, 64)
    sw16 = np.r_[8:16, 0:8]
    qd, kd = qkv[:, 0], qkv[:, 1]
    wdqk = np.concatenate([qd, qd[:, :, sw16], kd, kd[:, :, sw16]], axis=2).reshape(DM, 1280)
    wdv = qkv[:, 2].reshape(DM, 512)

    def fm(v):
        v = np.asarray(v, np.float32).reshape(-1)
        return v.reshape(-1, 128).T

    gains = np.concatenate([fm(g("ffn1_norm")), fm(g("mix_norm")), fm(g("mla_q_norm")), fm(g("mla_kv_norm")),
                            fm(g("mla_out_norm")), fm(g("dil_out_norm")), fm(g("ffn2_norm"))], axis=1)
    assert gains.shape == (128, NG)
    tabm, tabd = _rope_tables()
    jj = np.arange(128)[:, None]
    ii = np.arange(256)[None, :]
    mask = ((ii - jj >= 0) & (ii - jj <= 128)).astype(np.float32)
    mask = np.concatenate([mask, mask], axis=1).astype(ml_dtypes.bfloat16)
    ident = np.eye(128, dtype=np.float32).astype(ml_dtypes.bfloat16)
    c = np.ascontiguousarray
    shared = {
        "wg1": c(g("ffn1_w_gate")[0]), "wu1": c(g("ffn1_w_up")[0]), "wd1": c(g("ffn1_w_down")[0]),
        "wg2": c(g("ffn2_w_gate")[0]), "wu2": c(g("ffn2_w_up")[0]), "wd2": c(g("ffn2_w_down")[0]),
        "wlat": c(wlat), "wuq": c(wuq), "wuk": c(wuk), "wuv": c(wuv), "wdqk": c(wdqk), "wdv": c(wdv),
        "wout": c(g("w_out")[0]), "gains": c(gains), "gfin": c(g("final_norm").reshape(1, DM)),
        "tabm": tabm, "tabd": tabd, "mask": c(mask), "ident": c(ident),
    }
    return shared


def kernel(**inputs):
    x = np.asarray(inputs["x"], np.float32)
    B = x.shape[0]
    nseq = B // N_CORES
    shared = _host_layout(inputs)
    nc = build_program(nseq)
    in_maps = []
    for c in range(N_CORES):
        m = dict(shared)
        m["x"] = np.ascontiguousarray(x[c * nseq:(c + 1) * nseq])
        in_maps.append(m)
    res = run_bass_kernel_spmd(nc, in_maps, core_ids=list(range(N_CORES)))
    out = np.concatenate([np.asarray(r["y"], np.float32) for r in res.results], axis=0)
    return out
```
